# Optimizing a Trainium2 kernel written in Bass

```python
import jax, jax.numpy as jnp
from jax import lax
import numpy as np

D_MODEL = 1024
BATCH = 4
SEQ = 8192
DEPTH = 1

GLA_HEADS = 4
GLA_DK = 64
GLA_DV = 128
GLA_GATE_RANK = 16
GLA_GATE_TEMP = 16.0
GLA_CHUNK = 64
SWA_HEADS = 8
SWA_KV_HEADS = 2
SWA_HEAD_DIM = 64
SWA_WINDOW = 128
SWA_BLOCK = 128
PEER_HEADS = 8
PEER_N_KEYS = 128
PEER_N_EXPERTS = PEER_N_KEYS * PEER_N_KEYS
PEER_QUERY_DIM = 256
PEER_TOPK = 16
PEER_TOKEN_BLOCK = 128
RMS_EPS = 1e-6

GLA_QK_WIDTH = GLA_HEADS * GLA_DK
GLA_WIDTH = GLA_HEADS * GLA_DV
SWA_WIDTH = SWA_HEADS * SWA_HEAD_DIM
SWA_KV_WIDTH = SWA_KV_HEADS * SWA_HEAD_DIM
MIX_WIDTH = GLA_WIDTH + SWA_WIDTH
IN_WIDTH = 2 * GLA_QK_WIDTH + GLA_WIDTH + GLA_GATE_RANK + GLA_WIDTH + SWA_WIDTH + 2 * SWA_KV_WIDTH

kernel_name = "hymba_gla_swa_peer_adaln"


def rms_norm(t, g):
    tf = t.astype(jnp.float32)
    tf = tf * lax.rsqrt(jnp.mean(tf * tf, axis=-1, keepdims=True) + RMS_EPS)
    return (tf * g.astype(jnp.float32)).astype(t.dtype)


def modulate(t, g, shift, scale):
    return rms_norm(t, g) * (1 + scale[:, None, :]) + shift[:, None, :]


def gla_mixer(q, k, v, log_a):
    bsz, s = q.shape[0], q.shape[1]
    nc = s // GLA_CHUNK

    def to_chunks(t):
        return t.astype(jnp.float32).reshape(bsz, nc, GLA_CHUNK, GLA_HEADS, -1).transpose(1, 0, 3, 2, 4)

    causal = jnp.tril(jnp.ones((GLA_CHUNK, GLA_CHUNK), dtype=bool))

    def step(state, inp):
        qc, kc, vc, ac = inp
        b = jnp.cumsum(ac, axis=2)
        o_inter = jnp.einsum('bhtk,bhkv->bhtv', qc * jnp.exp(b), state)
        diff = b[:, :, :, None, :] - b[:, :, None, :, :]
        decay = jnp.exp(jnp.where(causal[None, None, :, :, None], diff, -jnp.inf))
        scores = jnp.einsum('bhtk,bhsk,bhtsk->bhts', qc, kc, decay)
        o_intra = jnp.einsum('bhts,bhsv->bhtv', scores, vc)
        b_last = b[:, :, -1:, :]
        state = jnp.exp(b_last[:, :, 0, :, None]) * state + jnp.einsum(
            'bhsk,bhsv->bhkv', kc * jnp.exp(b_last - b), vc)
        return state, o_inter + o_intra

    init = jnp.zeros((bsz, GLA_HEADS, GLA_DK, GLA_DV), jnp.float32)
    xs = (to_chunks(q * GLA_DK ** -0.5), to_chunks(k), to_chunks(v), to_chunks(log_a))
    _, o = lax.scan(step, init, xs)
    return o.transpose(1, 0, 3, 2, 4).reshape(bsz, s, GLA_HEADS, GLA_DV).astype(v.dtype)


def swa_mixer(q, k, v, sinks):
    bsz, s = q.shape[0], q.shape[1]
    nb = s // SWA_BLOCK
    grp = SWA_HEADS // SWA_KV_HEADS
    qb = q.reshape(bsz, nb, SWA_BLOCK, SWA_KV_HEADS, grp, SWA_HEAD_DIM) * SWA_HEAD_DIM ** -0.5

    def with_prev(t):
        tb = t.reshape(bsz, nb, SWA_BLOCK, SWA_KV_HEADS, SWA_HEAD_DIM)
        prev = jnp.pad(tb, ((0, 0), (1, 0), (0, 0), (0, 0), (0, 0)))[:, :-1]
        return jnp.concatenate([prev, tb], axis=2)

    kb, vb = with_prev(k), with_prev(v)
    sc = jnp.einsum('bnqhgd,bnkhd->bnhgqk', qb, kb).astype(jnp.float32)
    qi = jnp.arange(SWA_BLOCK)[:, None]
    kj = jnp.arange(2 * SWA_BLOCK)[None, :]
    dist = qi + SWA_BLOCK - kj
    key_pos = jnp.arange(nb)[:, None] * SWA_BLOCK - SWA_BLOCK + kj[0][None, :]
    valid = ((dist >= 0) & (dist < SWA_WINDOW))[None] & (key_pos >= 0)[:, None, :]
    slopes = (2.0 ** (-8.0 * (jnp.arange(SWA_HEADS) + 1) / SWA_HEADS)).astype(jnp.float32)
    slopes = slopes.reshape(SWA_KV_HEADS, grp)[None, None, :, :, None, None]
    sc = sc - slopes * dist.astype(jnp.float32)
    sc = jnp.where(valid[None, :, None, None], sc, -jnp.inf)
    sink = sinks.astype(jnp.float32).reshape(SWA_KV_HEADS, grp)[None, None, :, :, None, None]
    m = jnp.maximum(jnp.max(sc, axis=-1, keepdims=True), sink)
    p = jnp.exp(sc - m)
    p = p / (jnp.sum(p, axis=-1, keepdims=True) + jnp.exp(sink - m))
    o = jnp.einsum('bnhgqk,bnkhd->bnqhgd', p.astype(vb.dtype), vb)
    return o.reshape(bsz, s, SWA_WIDTH)


def peer_mixer(h, w_q, keys_1, keys_2, u_tab, v_tab):
    bsz, s, d = h.shape
    t = bsz * s
    hf = h.reshape(t, d)
    q = (hf @ w_q).reshape(t, PEER_HEADS, PEER_QUERY_DIM)
    half = PEER_QUERY_DIM // 2
    s1 = jnp.einsum('thd,nd->thn', q[..., :half], keys_1).astype(jnp.float32)
    s2 = jnp.einsum('thd,nd->thn', q[..., half:], keys_2).astype(jnp.float32)
    v1, i1 = lax.top_k(s1, PEER_TOPK)
    v2, i2 = lax.top_k(s2, PEER_TOPK)
    cand = (v1[..., :, None] + v2[..., None, :]).reshape(t, PEER_HEADS, PEER_TOPK * PEER_TOPK)
    sc, ci = lax.top_k(cand, PEER_TOPK)
    e1 = jnp.take_along_axis(i1, ci // PEER_TOPK, axis=-1)
    e2 = jnp.take_along_axis(i2, ci % PEER_TOPK, axis=-1)
    experts = (e1 * PEER_N_KEYS + e2).reshape(t, PEER_HEADS * PEER_TOPK)
    gates = jax.nn.softmax(sc, axis=-1).reshape(t, PEER_HEADS * PEER_TOPK)
    nblk = t // PEER_TOKEN_BLOCK

    def expert_block(args):
        hb, eb, gb = args
        a = jax.nn.gelu(jnp.einsum('td,ted->te', hb, u_tab[eb]).astype(jnp.float32), approximate=False)
        w = (gb * a).astype(hb.dtype)
        return jnp.einsum('te,ted->td', w, v_tab[eb])

    out = lax.map(expert_block, (hf.reshape(nblk, PEER_TOKEN_BLOCK, d),
                                 experts.reshape(nblk, PEER_TOKEN_BLOCK, -1),
                                 gates.reshape(nblk, PEER_TOKEN_BLOCK, -1)))
    return out.reshape(bsz, s, d)


def setup_inputs(seed: int = 0) -> dict:
    key = jax.random.key(seed)
    ks = jax.random.split(key, 24)
    f32 = jnp.float32
    n = lambda k, shape, sc: (jax.random.normal(k, shape, f32) * sc)
    gain = lambda k, shape: 1.0 + 0.02 * jax.random.normal(k, shape, f32)
    half = PEER_QUERY_DIM // 2
    return {
        "x": n(ks[0], (BATCH, SEQ, D_MODEL), 1.0),
        "c": n(ks[1], (BATCH, D_MODEL), 1.0),
        "w_ada": n(ks[2], (DEPTH, D_MODEL, 6 * D_MODEL), 0.5 * D_MODEL ** -0.5),
        "b_ada": n(ks[3], (DEPTH, 6 * D_MODEL), 0.02),
        "norm1_g": gain(ks[4], (DEPTH, D_MODEL)),
        "norm2_g": gain(ks[5], (DEPTH, D_MODEL)),
        "w_in": n(ks[6], (DEPTH, D_MODEL, IN_WIDTH), D_MODEL ** -0.5),
        "w_gla_alpha": n(ks[7], (DEPTH, GLA_GATE_RANK, GLA_QK_WIDTH), GLA_GATE_RANK ** -0.5),
        "b_gla_alpha": n(ks[8], (DEPTH, GLA_QK_WIDTH), 0.1),
        "gla_norm_g": gain(ks[9], (DEPTH, GLA_WIDTH)),
        "swa_sinks": n(ks[10], (DEPTH, SWA_HEADS), 0.5),
        "swa_norm_g": gain(ks[11], (DEPTH, SWA_WIDTH)),
        "w_out": n(ks[12], (DEPTH, MIX_WIDTH, D_MODEL), MIX_WIDTH ** -0.5),
        "w_peer_q": n(ks[13], (DEPTH, D_MODEL, PEER_HEADS * PEER_QUERY_DIM), D_MODEL ** -0.5),
        "peer_keys_1": n(ks[14], (DEPTH, PEER_N_KEYS, half), half ** -0.5),
        "peer_keys_2": n(ks[15], (DEPTH, PEER_N_KEYS, half), half ** -0.5),
        "peer_u": n(ks[16], (DEPTH, PEER_N_EXPERTS, D_MODEL), D_MODEL ** -0.5),
        "peer_v": n(ks[17], (DEPTH, PEER_N_EXPERTS, D_MODEL), 0.2),
        "final_g": gain(ks[18], (D_MODEL,)),
    }


def reference(x, c, w_ada, b_ada, norm1_g, norm2_g, w_in, w_gla_alpha, b_gla_alpha, gla_norm_g,
              swa_sinks, swa_norm_g, w_out, w_peer_q, peer_keys_1, peer_keys_2, peer_u, peer_v, final_g):
    bsz, s, _ = x.shape
    widths = (GLA_QK_WIDTH, GLA_QK_WIDTH, GLA_WIDTH, GLA_GATE_RANK, GLA_WIDTH, SWA_WIDTH, SWA_KV_WIDTH, SWA_KV_WIDTH)
    split_at = [int(v) for v in np.cumsum(widths)[:-1]]
    c_act = jax.nn.silu(c)
    for l in range(DEPTH):
        mod = c_act @ w_ada[l] + b_ada[l]
        shift1, scale1, gate1, shift2, scale2, gate2 = jnp.split(mod, 6, axis=-1)

        h = modulate(x, norm1_g[l], shift1, scale1)
        proj = h @ w_in[l]
        q_g, k_g, v_g, a_lr, r_g, q_s, k_s, v_s = jnp.split(proj, split_at, axis=-1)
        log_a = jax.nn.log_sigmoid((a_lr @ w_gla_alpha[l] + b_gla_alpha[l]).astype(jnp.float32)) / GLA_GATE_TEMP
        o_gla = gla_mixer(q_g.reshape(bsz, s, GLA_HEADS, GLA_DK),
                          k_g.reshape(bsz, s, GLA_HEADS, GLA_DK),
                          v_g.reshape(bsz, s, GLA_HEADS, GLA_DV),
                          log_a.reshape(bsz, s, GLA_HEADS, GLA_DK))
        o_gla = rms_norm(o_gla, gla_norm_g[l].reshape(GLA_HEADS, GLA_DV)).reshape(bsz, s, GLA_WIDTH)
        o_gla = o_gla * jax.nn.silu(r_g)
        o_swa = swa_mixer(q_s.reshape(bsz, s, SWA_HEADS, SWA_HEAD_DIM),
                          k_s.reshape(bsz, s, SWA_KV_HEADS, SWA_HEAD_DIM),
                          v_s.reshape(bsz, s, SWA_KV_HEADS, SWA_HEAD_DIM),
                          swa_sinks[l])
        o_swa = rms_norm(o_swa, swa_norm_g[l])
        y = jnp.concatenate([o_gla, o_swa], axis=-1) @ w_out[l]
        x = x + gate1[:, None, :] * y

        h2 = modulate(x, norm2_g[l], shift2, scale2)
        x = x + gate2[:, None, :] * peer_mixer(h2, w_peer_q[l], peer_keys_1[l], peer_keys_2[l], peer_u[l], peer_v[l])
    return rms_norm(x, final_g)
```

```python
import contextlib
import numpy as np
import concourse.bass as bass
import concourse.mybir as mybir
from concourse.bass_utils import run_bass_kernel_spmd

F32 = mybir.dt.float32
BF16 = mybir.dt.bfloat16
AF = mybir.ActivationFunctionType
ALU = mybir.AluOpType
AX = mybir.AxisListType

D = 1024
EPS = 1e-6
NEG = -30000.0


class R:
    __slots__ = ("name", "w", "rs")

    def __init__(self, name=""):
        self.name = name
        self.w = None
        self.rs = []


class Sched:
    ENGS = ("pe", "act", "dve", "pool", "sp")

    def __init__(self, nc, stack):
        self.nc = nc
        self.stack = stack
        self.sems = {}
        self.cnt = {}
        self.ops = {e: [] for e in self.ENGS}
        self.known = {e: {} for e in self.ENGS}
        for e in self.ENGS:
            self._mksem(e)
        self.nins = 0

    def _mksem(self, key):
        s = self.stack.enter_context(self.nc.semaphore("s%d" % len(self.sems)))
        self.sems[key] = s
        self.cnt[key] = 0

    def _need(self, eng, dep):
        if dep is None:
            return
        key, val = dep
        if key == eng and val > self.cnt[eng]:
            return
        if self.known[eng].get(key, 0) >= val:
            return
        self.known[eng][key] = val
        self.ops[eng].append(("wait", key, val))

    def _deps(self, eng, reads, writes):
        for r in reads:
            self._need(eng, r.w)
        for r in writes:
            self._need(eng, r.w)
            for d in r.rs:
                self._need(eng, d)

    def _mark(self, me, reads, writes):
        for r in reads:
            r.rs.append(me)
            if len(r.rs) > 24:
                r.rs = r.rs[-24:] if False else r.rs
        for r in writes:
            r.w = me
            r.rs = []

    def op(self, eng, fn, reads=(), writes=(), inc=True):
        self._deps(eng, reads, writes)
        if inc:
            self.cnt[eng] += 1
            self.ops[eng].append(("ins", fn, eng, 1))
            self._mark((eng, self.cnt[eng]), reads, writes)
        else:
            self.ops[eng].append(("ins", fn, eng, 0))
            self._mark((eng, self.cnt[eng] + 1), reads, writes)
        self.nins += 1

    def dma(self, fn, reads=(), writes=(), key=None, eng="sp"):
        self._deps(eng, reads, writes)
        sk = ("d", id(key))
        if sk not in self.sems:
            self._mksem(sk)
        self.cnt[sk] += 16
        self.ops[eng].append(("ins", fn, sk, 16))
        self._mark((sk, self.cnt[sk]), reads, writes)
        self.nins += 1

    def barrier(self):
        for e in self.ENGS:
            for k, v in self.cnt.items():
                if v > 0:
                    self._need(e, (k, v))

    def emit(self):
        nc = self.nc
        sems = self.sems

        def run(eng):
            def body(h):
                for item in self.ops[eng]:
                    if item[0] == "wait":
                        h.wait_ge(sems[item[1]], item[2])
                    elif item[3] == 0:
                        item[1](h)
                    else:
                        item[1](h).then_inc(sems[item[2]], item[3])
            return body

        with nc.Block() as block:
            block.tensor(run("pe"))
            block.scalar(run("act"))
            block.vector(run("dve"))
            block.gpsimd(run("pool"))
            block.sync(run("sp"))

    def mm(self, out, lhsT, rhs, start, stop, rd, wr, inc=None):
        if inc is None:
            inc = bool(stop)
        self.op("pe", lambda h: h.matmul(out, lhsT=lhsT, rhs=rhs, start=start, stop=stop), rd, wr, inc=inc)

    def tr(self, out, in_, ident, rd, wr):
        self.op("pe", lambda h: h.transpose(out=out, in_=in_, identity=ident), rd, wr)

    def act(self, out, in_, func, rd, wr, bias=None, scale=None, accum=None):
        kw = {}
        if bias is not None:
            kw["bias"] = bias
        if scale is not None:
            kw["scale"] = scale
        if accum is not None:
            kw["accum_out"] = accum
        self.op("act", lambda h: h.activation(out=out, in_=in_, func=func, **kw), rd, wr)

    def tt(self, eng, out, a, b, op, rd, wr):
        self.op(eng, lambda h: h.tensor_tensor(out=out, in0=a, in1=b, op=op), rd, wr)

    def ts(self, eng, out, a, s1, op0, rd, wr, s2=None, op1=None):
        if op1 is None:
            self.op(eng, lambda h: h.tensor_scalar(out=out, in0=a, scalar1=s1, scalar2=None, op0=op0), rd, wr)
        else:
            self.op(eng, lambda h: h.tensor_scalar(out=out, in0=a, scalar1=s1, scalar2=s2, op0=op0, op1=op1), rd, wr)

    def stt(self, out, a, scalar, b, op0, op1, rd, wr):
        self.op("dve", lambda h: h.scalar_tensor_tensor(out=out, in0=a, scalar=scalar, in1=b, op0=op0, op1=op1), rd, wr)

    def cp(self, eng, out, in_, rd, wr):
        if eng == "act":
            self.op("act", lambda h: h.copy(out=out, in_=in_), rd, wr)
        else:
            self.op(eng, lambda h: h.tensor_copy(out=out, in_=in_), rd, wr)

    def ld(self, out, in_, rd, wr, key):
        self.dma(lambda h: h.dma_start(out=out, in_=in_), rd, wr, key=key)


def build_nc(NT, NPRE, dbg=None, lvl=9):
    nc = bass.Bass("TRN2", target_bir_lowering=False)
    T = NT * 128
    TP = NPRE * 128
    dram = lambda name, shape, dt=F32, kind="ExternalInput": nc.dram_tensor(name, shape, dt, kind=kind).ap()
    x_d = dram("x", [T, D])
    xp_d = dram("xp", [TP, D])
    flag_d = dram("flag", [1])
    c_d = dram("c", [128, 8])
    wada_d = dram("w_ada", [D, 6 * D])
    bada_d = dram("b_ada", [1, 6 * D])
    n1g_d = dram("norm1_g", [D])
    n2g_d = dram("norm2_g", [D])
    win_d = dram("w_in", [D, 2320])
    walpha_d = dram("w_gla_alpha", [16, 256])
    balpha_d = dram("b_gla_alpha", [256])
    glag_d = dram("gla_norm_g", [512])
    sinks_d = dram("swa_sinks", [8])
    swag_d = dram("swa_norm_g", [512])
    wout_d = dram("w_out", [D, D])
    wq_d = dram("w_peer_q", [D, 2048])
    k1T_d = dram("peer_keys_1", [128, 128])
    k2T_d = dram("peer_keys_2", [128, 128])
    uT_d = dram("peer_u", [128, 128, D])
    v_d = dram("peer_v", [128, 128, D])
    fg_d = dram("final_g", [D])
    swab_d = dram("swa_bias", [128, 8 * 256])
    out_d = dram("out", [T, D], kind="ExternalOutput")
    x1s_d = dram("x1s", [NT, 128, D], kind="Internal")
    h2s_d = dram("h2s", [NT, 128, D], BF16, kind="Internal")
    ubf_d = dram("ubf", [128, 128, D], BF16, kind="Internal")
    vbf_d = dram("vbf", [128, 128, D], BF16, kind="Internal")
    dbg_d = {}
    if dbg:
        for k, shp in dbg.items():
            dbg_d[k] = dram("dbg_" + k, shp, kind="ExternalOutput")

    with contextlib.ExitStack() as st0:
        S = Sched(nc, st0)
        sb0 = lambda name, shape, dt=F32: st0.enter_context(nc.sbuf_tensor(name, shape, dt))
        r_out = R("out")
        r_dbg = R("dbg")

        def tap(name, ap, rd):
            if name in dbg_d:
                S.ld(dbg_d[name], ap, rd, [r_dbg], key=r_dbg)

        identf = sb0("identf", [128, 128]); identb = sb0("identb", [128, 128], BF16)
        onesf = sb0("onesf", [1, 128])
        gt2 = sb0("gt2", [128, D]); fgb = sb0("fgb", [128, D])
        fl = sb0("fl", [128, 1])
        rC = R("consts")
        S.op("pool", lambda h: h.memset(identf[:], 1.0), [], [rC])
        S.op("pool", lambda h: h.affine_select(out=identf[:], in_=identf[:], pattern=[[-1, 128]], compare_op=ALU.is_equal,
                                               fill=0.0, base=0, channel_multiplier=1), [rC], [rC])
        S.cp("pool", identb[:], identf[:], [rC], [rC])
        S.op("pool", lambda h: h.memset(onesf[:], 1.0), [], [rC])
        rfg = R("fgb"); rfl = R("fl")
        S.ld(fgb[:], fg_d.partition_broadcast(128), [], [rfg], rfg)
        S.ld(fl[:], flag_d.partition_broadcast(128), [], [rfl], rfl)
        rgt2 = R("gt2")
        cst = [sb0("cst%d" % i, [128, D]) for i in range(2)]; rcst = [R("cst0"), R("cst1")]
        cvb = [sb0("cvb%d" % i, [128, D], BF16) for i in range(2)]; rcv = [R("cv0"), R("cv1")]
        rub = [R("ub%d" % i) for i in range(128)]; rvb = [R("vb%d" % i) for i in range(128)]
        conv_state = {"n": 0}

        def conv_item(n):
            blk, which = n // 2, n % 2
            return ((uT_d, ubf_d, rub) if which == 0 else (v_d, vbf_d, rvb)), blk

        def conv_load(n):
            (src, dst, rr), blk = conv_item(n)
            s2 = n % 2
            S.dma((lambda o_, i_: lambda h: h.dma_start(out=o_, in_=i_))(cst[s2][:], src[blk]), [], [rcst[s2]],
                  key=rcst[s2], eng="pool")

        def conv_steps(k):
            for _ in range(k):
                n = conv_state["n"]
                if n >= 256:
                    return
                if n == 0:
                    conv_load(0)
                if n + 1 < 256:
                    conv_load(n + 1)
                (src, dst, rr), blk = conv_item(n)
                s2 = n % 2
                S.cp("dve", cvb[s2][:], cst[s2][:], [rcst[s2]], [rcv[s2]])
                S.dma((lambda o_, i_: lambda h: h.dma_start(out=o_, in_=i_))(dst[blk], cvb[s2][:]), [rcv[s2]], [rr[blk]],
                      key=rcv[s2], eng="pool")
                conv_state["n"] = n + 1

        with contextlib.ExitStack() as st:
            sb = lambda name, shape, dt=F32: st.enter_context(nc.sbuf_tensor(name, shape, dt))
            ps = lambda name, shape, dt=F32: st.enter_context(nc.psum_tensor(name, shape, dt))
            PA = ps("PA", [128, 1024], BF16); PB = ps("PB", [128, 512]); PC = ps("PC", [128, 512]); PD = ps("PD", [128, 512])
            PE_ = ps("PE", [128, 512]); PF = ps("PF", [128, 512]); PG = ps("PG", [128, 1024], BF16); PH = ps("PH", [128, 512])
            rPA, rPB, rPC, rPDs, rPDd, rPE, rPF, rPG, rPH = [R("P%d" % i) for i in range(9)]

            winb = sb("winb", [128, 8, 2320], BF16); woutb = sb("woutb", [128, 8, D], BF16)
            walb = sb("walb", [16, 256], BF16)
            gs1 = sb("gs1", [128, D]); sh1 = sb("sh1", [128, D]); gt1 = sb("gt1", [128, D])
            gs2 = sb("gs2", [128, D]); sh2 = sb("sh2", [128, D])
            ggla = sb("ggla", [128, 512]); gswa = sb("gswa", [128, 512])
            balb = sb("balb", [128, 256]); sinkb = sb("sinkb", [128, 8])
            swab = sb("swab", [128, 8, 256]); swab0 = sb("swab0", [128, 8, 256])
            tri2 = sb("tri2", [128, 64]); triblk = sb("triblk", [128, 128])
            fm = sb("fm", [128, 1])
            st_setup = contextlib.ExitStack()
            sbs = lambda name, shape, dt=F32: st_setup.enter_context(nc.sbuf_tensor(name, shape, dt))
            stg = sbs("stg", [128, 4096])
            ccol = sbs("ccol", [128, 8]); modrow = sbs("modrow", [1, 6 * D]); badar = sbs("badar", [1, 6 * D])
            rstg = R("stg"); rwin = R("win"); rwout = R("wout"); rwal = R("wal"); rbc = R("bc"); rmod = R("mod")

            for kc in range(8):
                S.ld(stg[:, 0:2320], win_d[kc * 128:(kc + 1) * 128, :], [], [rstg], rstg)
                S.cp("dve" if kc % 2 == 0 else "pool", winb[:, kc, :], stg[:, 0:2320], [rstg], [rwin])
            for kc in range(8):
                S.ld(stg[:, 0:D], wout_d[kc * 128:(kc + 1) * 128, :], [], [rstg], rstg)
                S.cp("dve" if kc % 2 == 0 else "pool", woutb[:, kc, :], stg[:, 0:D], [rstg], [rwout])
            S.ld(stg[0:16, 0:256], walpha_d[:, :], [], [rstg], rstg)
            S.cp("dve", walb[:], stg[0:16, 0:256], [rstg], [rwal])
            S.ld(ggla[:], glag_d.partition_broadcast(128), [], [rbc], rbc)
            S.ld(gswa[:], swag_d.partition_broadcast(128), [], [rbc], rbc)
            S.ld(balb[:], balpha_d.partition_broadcast(128), [], [rbc], rbc)
            S.ld(sinkb[:], sinks_d.partition_broadcast(128), [], [rbc], rbc)
            S.ld(swab[:].rearrange("p a b -> p (a b)"), swab_d[:, :], [], [rbc], rbc)
            S.ld(gs1[:], n1g_d.partition_broadcast(128), [], [rbc], rbc)
            S.ld(gs2[:], n2g_d.partition_broadcast(128), [], [rbc], rbc)
            S.ld(ccol[:], c_d[:, :], [], [rbc], rbc)
            S.ld(badar[:], bada_d[:, :], [], [rbc], rbc)
            S.op("pool", lambda h: h.memset(triblk[:], 1.0), [rC], [rC])
            S.op("pool", lambda h: h.affine_select(out=triblk[:], in_=triblk[:], pattern=[[1, 128]], compare_op=ALU.is_ge,
                                                   fill=0.0, base=0, channel_multiplier=-1), [rC], [rC])
            S.op("pool", lambda h: h.memset(triblk[0:64, 64:128], 0.0), [rC], [rC])
            S.cp("pool", tri2[0:64, :], triblk[0:64, 0:64], [rC], [rC])
            S.cp("pool", tri2[64:128, :], triblk[64:128, 64:128], [rC], [rC])
            S.ts("dve", fm[:], fl[:], -1.0, ALU.add, [rfl], [rC], s2=-NEG, op1=ALU.mult)
            S.ts("dve", swab0[:, :, 0:128], swab[:, :, 0:128], fm[:, 0:1], ALU.add, [rbc, rC], [rC])
            S.cp("dve", swab0[:, :, 128:256], swab[:, :, 128:256], [rbc, rC], [rC])
            S.act(ccol[:], ccol[:], AF.Silu, [rbc], [rbc])
            wada_v = wada_d.rearrange("(kc p) n -> p kc n", p=128)
            for nb in range(12):
                S.ld(stg[:].rearrange("p (a b) -> p a b", a=8), wada_v[:, :, nb * 512:(nb + 1) * 512], [], [rstg], rstg)
                for kc in range(8):
                    S.mm(PB[0:1, :], ccol[:, kc:kc + 1], stg[:, kc * 512:(kc + 1) * 512], kc == 0, kc == 7, [rbc, rstg], [rPB])
                S.tt("dve", modrow[0:1, nb * 512:(nb + 1) * 512], PB[0:1, :], badar[0:1, nb * 512:(nb + 1) * 512], ALU.add,
                     [rPB, rbc], [rmod])
            segs = [sh1, gs1, gt1, sh2, gs2, gt2]
            for sg in range(6):
                for hf in range(2):
                    S.mm(PC[:, :], onesf[0:1, :], modrow[0:1, sg * D + hf * 512: sg * D + (hf + 1) * 512], True, True,
                         [rC, rmod], [rPC])
                    dst = segs[sg][:, hf * 512:(hf + 1) * 512]
                    if sg in (1, 4):
                        S.stt(dst, PC[:, :], 1.0, dst, ALU.add, ALU.mult, [rPC, rbc], [rbc, rgt2])
                    else:
                        S.cp("act", dst, PC[:, :], [rPC], [rbc, rgt2])

            S.barrier()
            st_setup.close()
            xt = [sb("xt%d" % i, [128, D]) for i in range(2)]; rxt = [R("xt0"), R("xt1")]
            jnk = sb("jnk", [128, D]); rjnk = R("jnk")
            t1 = sb("t1", [128, D]); rt1 = R("t1")
            hb = sb("hb", [128, D], BF16); rhb = R("hb")
            hT = sb("hT", [128, 8, 128], BF16); rhT = R("hT")
            st8 = sb("st8", [128, 16]); rst8 = R("st8")
            aT = sb("aT", [16, 128], BF16); raT = R("aT")
            la = sb("la", [128, 256]); rla = R("la")
            eb = sb("eb", [128, 2, 128]); enb = sb("enb", [128, 2, 128]); reb = R("eb")
            qeT = sb("qeT", [128, 2, 128], BF16); keT = sb("keT", [128, 2, 128], BF16); rqe = R("qe"); rke = R("ke")
            ke = sb("ke", [128, 256], BF16); rket = R("ket")
            vgb = sb("vgb", [128, 512], BF16); rvg = R("vg")
            sr = sb("sr", [128, 512]); rsr = R("sr")
            rgl = sb("rgl", [128, 512]); rrgl = R("rgl")
            scm = sb("scm", [128, 4, 64], BF16); rscm = R("scm")
            Sf = sb("Sf", [128, 2, 128]); Sb = sb("Sb", [128, 2, 128], BF16); rS = R("S"); rSb = R("Sb")
            t1g = sb("t1g", [128, 512]); t2g = sb("t2g", [128, 512]); rt1g = R("t1g"); rt2g = R("t2g")
            og = sb("og", [128, 512]); rog = R("og")
            ocat = sb("ocat", [128, D], BF16); roc = R("ocat")
            qsT = sb("qsT", [128, 4, 128], BF16); rqs = R("qs")
            ksT = [sb("ksT%d" % i, [128, 128], BF16) for i in range(2)]; rks = [R("ks0"), R("ks1")]
            vsb = [sb("vsb%d" % i, [128, 128], BF16) for i in range(2)]; rvs = [R("vs0"), R("vs1")]
            sc2 = sb("sc2", [128, 8, 256]); rsc2 = R("sc2")
            pb = sb("pb", [128, 8, 256], BF16); rpb = R("pb")
            pT = sb("pT", [128, 8, 128], BF16); rpT = R("pT")
            osw = sb("osw", [128, 512]); rosw = R("osw")
            ocT = sb("ocT", [128, 8, 128], BF16); rocT = R("ocT")
            x1 = sb("x1", [128, D]); rx1 = R("x1")
            h2b = sb("h2b", [128, D], BF16); rh2b = R("h2b")
            h2T = sb("h2T", [128, 8, 128], BF16); rh2T = R("h2T")
            rx1s = [R("x1s%d" % i) for i in range(NT)]
            rh2s = [R("h2s%d" % i) for i in range(NT)]

            S.op("pool", lambda h: h.memset(Sf[:], 0.0), [], [rS])
            S.op("pool", lambda h: h.memset(Sb[:], 0.0), [], [rSb])
            for i in range(2):
                S.op("pool", (lambda i: lambda h: h.memset(ksT[i][:], 0.0))(i), [], [rks[i]])
                S.op("pool", (lambda i: lambda h: h.memset(vsb[i][:], 0.0))(i), [], [rvs[i]])

            def rstd_cols(dst, src, n, rr):
                S.act(dst, src, AF.Ln, [rr], [rr], bias=EPS, scale=1.0 / n)
                S.act(dst, dst, AF.Exp, [rr], [rr], scale=-0.5)

            def modulate(src, rsrc, gs, sh, dstb, rdst, dstT, rdstT):
                S.act(jnk[:], src, AF.Square, [rsrc], [rjnk, rst8], accum=st8[:, 0:1])
                rstd_cols(st8[:, 2:3], st8[:, 0:1], D, rst8)
                S.stt(t1[:], src, st8[:, 2:3], gs[:], ALU.mult, ALU.mult, [rsrc, rst8, rbc], [rt1])
                S.tt("dve", dstb[:], t1[:], sh[:], ALU.add, [rt1, rbc], [rdst])
                for kc in range(8):
                    S.tr(PA[:, kc * 128:(kc + 1) * 128], dstb[:, kc * 128:(kc + 1) * 128], identb[:], [rdst, rC], [rPA])
                S.cp("act", dstT[:].rearrange("p a b -> p (a b)"), PA[:, :], [rPA], [rdstT])

            def proj_fm(dst, col0, ncols, rw):
                for kc in range(8):
                    S.mm(dst, winb[:, kc, col0:col0 + ncols], hT[:, kc, :], kc == 0, kc == 7, [rwin, rhT], [rw])

            def proj_tm(dst, col0, ncols, rw):
                for kc in range(8):
                    S.mm(dst, hT[:, kc, :], winb[:, kc, col0:col0 + ncols], kc == 0, kc == 7, [rwin, rhT], [rw])

            def tile_step(g, src_d, full, want_kv, own_idx):
                slot = g % 2
                prev = (g + 1) % 2
                xs = xt[g % 2]; rxs = rxt[g % 2]
                S.ld(xs[:], src_d, [], [rxs], rxs)
                modulate(xs[:], rxs, gs1, sh1, hb, rhb, hT, rhT)
                proj_fm(PB[:, 0:128], 256, 128, rPB)
                proj_fm(PB[:, 128:256], 384, 128, rPB)
                if full:
                    proj_fm(PB[:, 256:384], 0, 128, rPB)
                    proj_fm(PB[:, 384:512], 128, 128, rPB)
                proj_fm(PH[0:16, 128:256], 1024, 16, rPH)
                S.cp("dve", aT[:, :], PH[0:16, 128:256], [rPH], [raT])
                if want_kv:
                    proj_fm(PH[:, 0:128], 2064, 128, rPH)
                    S.cp("act", ksT[slot][:], PH[:, 0:128], [rPH], [rks[slot]])
                    proj_tm(PH[:, 256:384], 2192, 128, rPH)
                    S.cp("dve", vsb[slot][:], PH[:, 256:384], [rPH], [rvs[slot]])
                proj_tm(PE_[:, :], 512, 512, rPE)
                S.cp("act", vgb[:], PE_[:, :], [rPE], [rvg])
                S.mm(PF[:, 0:256], aT[0:16, :], walb[0:16, :], True, True, [raT, rwal], [rPF])
                S.tt("dve", la[:], PF[:, 0:256], balb[:], ALU.add, [rPF, rbc], [rla])
                S.act(la[:], la[:], AF.Exp, [rla], [rla], scale=-1.0)
                S.act(la[:], la[:], AF.Ln, [rla], [rla], bias=1.0)
                for p in range(2):
                    S.mm(PF[:, 256 + p * 128:256 + (p + 1) * 128], la[:, p * 128:(p + 1) * 128], triblk[:], True, True,
                         [rla, rC], [rPF])
                S.act(eb[:].rearrange("p a b -> p (a b)"), PF[:, 256:512], AF.Exp, [rPF], [reb], scale=-1.0 / 16.0)
                S.act(enb[:].rearrange("p a b -> p (a b)"), PF[:, 256:512], AF.Exp, [rPF], [reb], scale=1.0 / 16.0)
                S.tt("dve", keT[:].rearrange("p a b -> p (a b)"), PB[:, 0:256], enb[:].rearrange("p a b -> p (a b)"), ALU.mult,
                     [rPB, reb], [rke])
                if full:
                    S.stt(qeT[:].rearrange("p a b -> p (a b)"), PB[:, 256:512], 0.125, eb[:].rearrange("p a b -> p (a b)"),
                          ALU.mult, ALU.mult, [rPB, reb], [rqe])
                for p in range(2):
                    S.tr(PG[:, p * 128:(p + 1) * 128], keT[:, p, :], identb[:], [rke, rC], [rPG])
                S.cp("dve", ke[:], PG[:, 0:256], [rPG], [rket])
                if full:
                    if lvl < 3.01:
                        return
                    proj_fm(PC[:, 0:128], 1552, 128, rPC); proj_fm(PC[:, 128:256], 1680, 128, rPC)
                    proj_fm(PC[:, 256:384], 1808, 128, rPC); proj_fm(PC[:, 384:512], 1936, 128, rPC)
                    S.cp("act", qsT[:].rearrange("p a b -> p (a b)"), PC[:, :], [rPC], [rqs])
                    if lvl < 3.02:
                        return
                    proj_tm(PC[:, :], 1040, 512, rPC)
                    if lvl < 3.021:
                        return
                    S.tt("dve", rgl[:], PC[:, :], ggla[:], ALU.mult, [rPC, rbc], [rrgl])
                    if lvl < 3.022:
                        return
                    S.act(sr[:], PC[:, :], AF.Exp, [rPC, rrgl], [rsr], scale=-1.0)
                    if lvl < 3.023:
                        return
                    S.act(sr[:], sr[:], AF.Ln, [rsr], [rsr], bias=1.0)
                    if lvl < 3.024:
                        return
                    S.act(sr[:], sr[:], AF.Exp, [rsr], [rsr], scale=-1.0)
                    if lvl < 3.03:
                        return
                    for c in range(2):
                        for hh in range(4):
                            p, o = hh // 2, 64 * (hh % 2)
                            S.mm(PD[c * 64:(c + 1) * 64, hh * 64:(hh + 1) * 64], keT[o:o + 64, p, c * 64:(c + 1) * 64],
                                 qeT[o:o + 64, p, c * 64:(c + 1) * 64], True, True, [rke, rqe], [rPDs])
                    S.tt("dve", scm[:], PD[:, 0:256].rearrange("p (a b) -> p a b", a=4),
                         tri2[:].unsqueeze(1).to_broadcast([128, 4, 64]), ALU.mult, [rPDs, rC], [rscm])
                if full and lvl < 3.04:
                    return
                for c in range(2):
                    cs = slice(c * 64, (c + 1) * 64)
                    if full:
                        for hh in range(4):
                            p, o = hh // 2, 64 * (hh % 2)
                            S.mm(PC[cs, hh * 128:(hh + 1) * 128], qeT[o:o + 64, p, cs], Sb[o:o + 64, p, :], True, True,
                                 [rqe, rSb], [rPC])
                            S.mm(PE_[cs, hh * 128:(hh + 1) * 128], scm[cs, hh, :], vgb[cs, hh * 128:(hh + 1) * 128], True, True,
                                 [rscm, rvg], [rPE])
                    for hh in range(4):
                        p, o = hh // 2, 64 * (hh % 2)
                        S.mm(PD[o:o + 64, 256 + p * 128:256 + (p + 1) * 128], ke[cs, hh * 64:(hh + 1) * 64],
                             vgb[cs, hh * 128:(hh + 1) * 128], True, True, [rket, rvg], [rPDd])
                    for p in range(2):
                        el = eb[:, p, c * 64 + 63:c * 64 + 64]
                        S.ts("dve", Sf[:, p, :], Sf[:, p, :], el, ALU.mult, [rS, reb], [rS])
                        S.stt(Sf[:, p, :], PD[:, 256 + p * 128:256 + (p + 1) * 128], el, Sf[:, p, :], ALU.mult, ALU.add,
                              [rPDd, reb, rS], [rS])
                    S.cp("act", Sb[:], Sf[:], [rS], [rSb])
                if not full:
                    return
                if lvl < 3.2:
                    return
                S.cp("act", t2g[:], PC[:, :], [rPC], [rt2g])
                S.tt("dve", og[:], PE_[:, :], t2g[:], ALU.add, [rPE, rt2g], [rog])
                S.act(t1g[:], og[:], AF.Square, [rog], [rt1g])
                S.op("dve", lambda h: h.reduce_sum(out=st8[:, 4:8], in_=t1g[:].rearrange("p (a b) -> p a b", a=4), axis=AX.X),
                     [rt1g], [rst8])
                rstd_cols(st8[:, 12:16], st8[:, 4:8], 128.0, rst8)
                S.tt("dve", t1g[:].rearrange("p (a b) -> p a b", a=4), og[:].rearrange("p (a b) -> p a b", a=4),
                     st8[:, 12:16].unsqueeze(2).to_broadcast([128, 4, 128]), ALU.mult, [rog, rst8, rt1g], [rt1g])
                S.tt("dve", t2g[:], sr[:], rgl[:], ALU.mult, [rsr, rrgl], [rt2g])
                S.tt("dve", ocat[:, 0:512], t1g[:], t2g[:], ALU.mult, [rt1g, rt2g], [roc])
                if lvl < 3.3:
                    return
                bias_t = swab0 if own_idx == 0 else swab
                for pr in range(4):
                    for hl in range(2):
                        hd = pr * 2 + hl
                        cc, o = hd % 4, 64 * (hd // 4)
                        S.mm(PF[:, hl * 256:hl * 256 + 128], qsT[o:o + 64, cc, :], ksT[prev][o:o + 64, :], True, True,
                             [rqs, rks[prev]], [rPF])
                        S.mm(PF[:, hl * 256 + 128:hl * 256 + 256], qsT[o:o + 64, cc, :], ksT[slot][o:o + 64, :], True, True,
                             [rqs, rks[slot]], [rPF])
                    S.stt(sc2[:, pr * 2:pr * 2 + 2, :].rearrange("p a b -> p (a b)"), PF[:, :], 0.125,
                          bias_t[:, pr * 2:pr * 2 + 2, :].rearrange("p a b -> p (a b)"), ALU.mult, ALU.add, [rPF, rbc, rC], [rsc2])
                sw = sb0s["sw"]
                S.op("dve", lambda h: h.reduce_max(out=sw[:, 0:8], in_=sc2[:], axis=AX.X), [rsc2], [rsw])
                S.tt("dve", sw[:, 0:8], sw[:, 0:8], sinkb[:], ALU.max, [rsw, rbc], [rsw])
                S.ts("dve", sw[:, 8:16], sw[:, 0:8], -1.0, ALU.mult, [rsw], [rsw])
                for hd in range(8):
                    S.act(pb[:, hd, :], sc2[:, hd, :], AF.Exp, [rsc2, rsw], [rpb, rsw], bias=sw[:, 8 + hd:9 + hd], scale=1.0,
                          accum=sw[:, 16 + hd:17 + hd])
                S.tt("dve", sw[:, 24:32], sinkb[:], sw[:, 0:8], ALU.subtract, [rsw, rbc], [rsw])
                S.act(sw[:, 24:32], sw[:, 24:32], AF.Exp, [rsw], [rsw])
                S.tt("dve", sw[:, 24:32], sw[:, 24:32], sw[:, 16:24], ALU.add, [rsw], [rsw])
                S.op("dve", lambda h: h.reciprocal(out=sw[:, 32:40], in_=sw[:, 24:32]), [rsw], [rsw])
                if lvl < 3.4:
                    return
                for hg in range(2):
                    for hl in range(4):
                        hd = hg * 4 + hl
                        for hf in range(2):
                            S.tr(PG[:, (hl * 2 + hf) * 128:(hl * 2 + hf + 1) * 128], pb[:, hd, hf * 128:(hf + 1) * 128], identb[:],
                                 [rpb, rC], [rPG])
                    S.cp("act" if hg == 0 else "dve", pT[:].rearrange("p a b -> p (a b)"), PG[:, :], [rPG], [rpT])
                    for hl in range(4):
                        hd = hg * 4 + hl
                        kv = hd // 4
                        S.mm(PH[:, hd * 64:(hd + 1) * 64], pT[:, hl * 2, :], vsb[prev][:, kv * 64:(kv + 1) * 64], True, False,
                             [rpT, rvs[prev]], [rPH])
                        S.mm(PH[:, hd * 64:(hd + 1) * 64], pT[:, hl * 2 + 1, :], vsb[slot][:, kv * 64:(kv + 1) * 64], False, True,
                             [rpT, rvs[slot]], [rPH])
                S.tt("dve", osw[:].rearrange("p (a b) -> p a b", a=8), PH[:, :].rearrange("p (a b) -> p a b", a=8),
                     sw[:, 32:40].unsqueeze(2).to_broadcast([128, 8, 64]), ALU.mult, [rPH, rsw], [rosw])
                S.act(t2g[:], osw[:], AF.Square, [rosw, roc], [rt2g, rsw], accum=sw[:, 40:41])
                rstd_cols(sw[:, 42:43], sw[:, 40:41], 512.0, rsw)
                S.stt(ocat[:, 512:1024], osw[:], sw[:, 42:43], gswa[:], ALU.mult, ALU.mult, [rosw, rsw, rbc], [roc])
                if lvl < 3.5:
                    return
                for kc in range(8):
                    S.tr(PA[:, kc * 128:(kc + 1) * 128], ocat[:, kc * 128:(kc + 1) * 128], identb[:], [roc, rC], [rPA])
                S.cp("act", ocT[:].rearrange("p a b -> p (a b)"), PA[:, :], [rPA], [rocT])
                for hf, (Pq, rq) in enumerate(((PB, rPB), (PC, rPC))):
                    for kc in range(8):
                        S.mm(Pq[:, :], ocT[:, kc, :], woutb[:, kc, hf * 512:(hf + 1) * 512], kc == 0, kc == 7, [rocT, rwout], [rq])
                    S.tt("dve", t1[:, hf * 512:(hf + 1) * 512], Pq[:, :], gt1[:, hf * 512:(hf + 1) * 512], ALU.mult,
                         [rq, rbc], [rt1])
                S.tt("dve", x1[:], t1[:], xs[:], ALU.add, [rt1, rxs], [rx1])
                S.ld(x1s_d[own_idx], x1[:], [rx1], [rx1s[own_idx]], rx1)
                if lvl < 3.6:
                    return
                modulate(x1[:], rx1, gs2, sh2, h2b, rh2b, h2T, rh2T)
                S.ld(h2s_d[own_idx].rearrange("p (a b) -> p a b", a=8), h2T[:], [rh2T], [rh2s[own_idx]], rh2T)
                if own_idx == 0:
                    tap("x1", x1[:], [rx1])
                    tap("ocat", t1g[:], [rt1g])

            sb0s = {"sw": sb("sw", [128, 48])}
            rsw = R("sw")
            xp_v = xp_d.rearrange("(n p) f -> n p f", p=128)
            x_v = x_d.rearrange("(n p) f -> n p f", p=128)
            per_tile = -(-256 // (NPRE + NT))
            for g in range(NPRE if lvl >= 2 else 0):
                tile_step(g, xp_v[g], False, g == NPRE - 1, None)
                conv_steps(per_tile)
            for p in range(2):
                S.ts("dve", Sf[:, p, :], Sf[:, p, :], fl[:, 0:1], ALU.mult, [rS, rfl], [rS])
            S.cp("pool", Sb[:], Sf[:], [rS], [rSb])
            for i in range(NT if lvl >= 3 else 0):
                tile_step(NPRE + i, x_v[i], True, True, i)
                conv_steps(per_tile)
            conv_steps(256)
            S.barrier()

        with contextlib.ExitStack() as st:
            sb = lambda name, shape, dt=F32: st.enter_context(nc.sbuf_tensor(name, shape, dt))
            ps = lambda name, shape, dt=F32: st.enter_context(nc.psum_tensor(name, shape, dt))
            OUT = [ps("OUT%d" % i, [128, D]) for i in range(2)]; rOUT = [R("OUT0"), R("OUT1")]
            AT = [ps("AT%d" % i, [128, 512]) for i in range(2)]; rAT = [R("AT0"), R("AT1")]
            GT = [ps("GT%d" % i, [128, 512]) for i in range(2)]; rGT = [R("GT0"), R("GT1")]
            wqb = sb("wqb", [128, 8, 2048], BF16); rwq = R("wq")
            k1b = sb("k1b", [128, 128], BF16); k2b = sb("k2b", [128, 128], BF16); rkk = R("kk")
            stg = cst; rstg = rcst
            for kc2 in range(16):
                kc, hf = kc2 // 2, kc2 % 2
                S.ld(stg[kc2 % 2][:], wq_d[kc * 128:(kc + 1) * 128, hf * 1024:(hf + 1) * 1024], [], [rstg[kc2 % 2]], rstg[kc2 % 2])
                S.cp("dve" if kc2 % 2 == 0 else "pool", wqb[:, kc, hf * 1024:(hf + 1) * 1024], stg[kc2 % 2][:], [rstg[kc2 % 2]], [rwq])
            S.ld(stg[0][:, 0:128], k1T_d[:, :], [], [rstg[0]], rstg[0])
            S.ld(stg[1][:, 0:128], k2T_d[:, :], [], [rstg[1]], rstg[1])
            S.cp("dve", k1b[:], stg[0][:, 0:128], [rstg[0]], [rkk])
            S.cp("dve", k2b[:], stg[1][:, 0:128], [rstg[1]], [rkk])
            h2c = sb("h2c", [128, 8, 256], BF16); rh2c = R("h2c")
            fbuf = [sb("fbuf%d" % i, [128, 8, 1024], BF16) for i in range(2)]; rfb = [R("fb0"), R("fb1")]
            fl0 = fbuf[0][:].rearrange("p a b -> p (a b)"); fl1 = fbuf[1][:].rearrange("p a b -> p (a b)")
            ssb = [fl0[:, tt * 4096:(tt + 1) * 4096].bitcast(F32).rearrange("p (a b) -> p a b", a=16) for tt in range(2)]
            rss = [R("ss0"), R("ss1")]
            qT = fl1[:, 0:4096].rearrange("p (a b) -> p a b", a=16); rqT = R("qT")
            wk = fl1[:, 4096:5248].bitcast(F32); rwk = R("wk")
            cnd = fl1[:, 5248:6400].bitcast(F32); rcnd = R("cnd")
            vv = fl1[:, 6400:7168].bitcast(F32).rearrange("p (a b c) -> p a b c", a=8, b=2); rvv = R("vv")
            ctop = fl1[:, 7168:7552].bitcast(F32).rearrange("p (a b) -> p a b", a=8); rct = R("ctop")
            e16 = fl1[:, 7552:7808].bitcast(F32).rearrange("p (a b) -> p a b", a=8); re16 = R("e16")
            alias_rs = [rss[0], rss[1], rqT, rwk, rcnd, rvv, rct, re16]
            fence = sb("fence", [128, 2]);
            P1 = [sb("P1_%d" % i, [128, 8, 128]) for i in range(2)]; P2 = [sb("P2_%d" % i, [128, 8, 128]) for i in range(2)]
            Kf = [sb("Kf%d" % i, [128, 8]) for i in range(2)]; rgp = [R("gp0"), R("gp1")]
            sm = sb("sm", [128, 64]); rsm = R("sm")
            xd = [sb("xd%d" % i, [128, 1024]) for i in range(2)]; rxd = [R("xd0"), R("xd1")]
            xa = [sb("xa%d" % i, [128, 1024]) for i in range(6)]; rxa = [R("xa%d" % i) for i in range(6)]
            xbuf = [xd[0], xd[1], xa[0]]; rxb = [rxd[0], rxd[1], rxa[0]]
            uq = [sb("uq%d" % i, [128, 4, D], BF16) for i in range(3)]; ruq = [R("uq%d" % i) for i in range(3)]
            vq = [sb("vq%d" % i, [128, 4, D], BF16) for i in range(3)]; rvq = [R("vq%d" % i) for i in range(3)]
            ga = [sb("ga%d" % i, [128, 512]) for i in range(2)]; rga = [R("ga0"), R("ga1")]
            WT = [sb("WT%d" % i, [128, 4, 128], BF16) for i in range(2)]; rWT = [R("WT0"), R("WT1")]
            x1l, yo, jn = xbuf[0], xbuf[1], xbuf[2]
            rx1l, ryo, rjn = rxb[0], rxb[1], rxb[2]
            fs = sb("fs", [128, 8]); rfs = R("fs")

            print("[kernel] phase B sbuf bytes remaining:", nc.sbuf_bytes_remaining, flush=True)
            NCH = NT // 2 if lvl >= 5 else 0
            cnt_e = 0
            cnt_x = 0
            cnt_a = 0
            for ch in range(NCH):
                for tt in range(2):
                    S.ld(h2c[:, :, tt * 128:(tt + 1) * 128], h2s_d[ch * 2 + tt].rearrange("p (a b) -> p a b", a=8),
                         [rh2s[ch * 2 + tt]], [rh2c], rh2c)
                S.op("dve", lambda h: h.memset(fence[:, 0:1], 0.0), [rfb[0], rfb[1]], alias_rs + [rfb[0], rfb[1]])
                for cc in range(16):
                    a = cc % 2
                    for kc in range(8):
                        S.mm(AT[a][:, 0:256], wqb[:, kc, cc * 128:(cc + 1) * 128], h2c[:, kc, :], kc == 0, kc == 7,
                             [rwq, rh2c], [rAT[a]])
                    S.cp("act" if cc % 2 == 0 else "dve", qT[:, cc, :], AT[a][:, 0:256], [rAT[a]], [rqT])
                for tt in range(2):
                    for q4 in range(4):
                        a = q4 % 2
                        for j in range(4):
                            cc = q4 * 4 + j
                            S.mm(GT[a][:, j * 128:(j + 1) * 128], qT[:, cc, tt * 128:(tt + 1) * 128], (k1b if cc % 2 == 0 else k2b)[:],
                                 True, True, [rqT, rkk], [rGT[a]])
                        S.cp("act" if q4 % 2 == 0 else "dve", ssb[tt][:, q4 * 4:(q4 + 1) * 4, :].rearrange("p a b -> p (a b)"),
                             GT[a][:, :], [rGT[a]], [rss[tt]])
                    for hd in range(8):
                        for hf in range(2):
                            src = ssb[tt][:, hd * 2 + hf, :]
                            dst = vv[:, hd, hf, :]
                            S.op("dve", (lambda d_, s_: lambda h: h.max(out=d_, in_=s_))(dst[:, 0:8], src), [rss[tt]], [rvv])
                            S.op("dve", (lambda d_, s_: lambda h: h.match_replace(out=wk[:, 0:128], in_to_replace=d_, in_values=s_,
                                                                                   imm_value=-1e30))(dst[:, 0:8], src), [rss[tt], rvv], [rwk])
                            S.op("dve", (lambda d_: lambda h: h.max(out=d_, in_=wk[:, 0:128]))(dst[:, 8:16]), [rwk], [rvv])
                            S.op("dve", (lambda d_: lambda h: h.match_replace(out=wk[:, 0:128], in_to_replace=d_, in_values=wk[:, 0:128],
                                                                              imm_value=-1e30))(dst[:, 8:16]), [rwk, rvv], [rwk])
                            S.op("dve", (lambda d_: lambda h: h.max(out=d_, in_=wk[:, 0:128]))(dst[:, 16:24]), [rwk], [rvv])
                    for hd in range(8):
                        S.tt("pool", cnd[:].rearrange("p (a b) -> p a b", a=24),
                             vv[:, hd, 0, :].unsqueeze(2).to_broadcast([128, 24, 24]),
                             vv[:, hd, 1, :].unsqueeze(1).to_broadcast([128, 24, 24]), ALU.add, [rvv], [rcnd])
                        dst = ctop[:, hd, :]
                        S.op("dve", (lambda d_: lambda h: h.max(out=d_, in_=cnd[:]))(dst[:, 0:8]), [rcnd], [rct])
                        S.op("dve", (lambda d_: lambda h: h.match_replace(out=wk[:], in_to_replace=d_, in_values=cnd[:],
                                                                          imm_value=-1e30))(dst[:, 0:8]), [rcnd, rct], [rwk])
                        S.op("dve", (lambda d_: lambda h: h.max(out=d_, in_=wk[:]))(dst[:, 8:16]), [rwk], [rct])
                        S.op("dve", (lambda d_: lambda h: h.match_replace(out=wk[:], in_to_replace=d_, in_values=wk[:],
                                                                          imm_value=-1e30))(dst[:, 8:16]), [rwk, rct], [rwk])
                        S.op("dve", (lambda d_: lambda h: h.max(out=d_, in_=wk[:]))(dst[:, 16:24]), [rwk], [rct])
                    S.tt("dve", sm[:, 0:8], ctop[:, :, 15], ctop[:, :, 16], ALU.add, [rct], [rsm])
                    S.ts("dve", sm[:, 0:8], sm[:, 0:8], 0.5, ALU.mult, [rsm], [rsm])
                    S.cp("dve", sm[:, 8:16], ctop[:, :, 0], [rct], [rsm])
                    S.tt("dve", e16[:], ctop[:, :, 0:16], sm[:, 8:16].unsqueeze(2).to_broadcast([128, 8, 16]), ALU.subtract,
                         [rct, rsm], [re16])
                    S.act(e16[:], e16[:], AF.Exp, [re16], [re16])
                    S.op("dve", lambda h: h.reduce_sum(out=sm[:, 16:24], in_=e16[:], axis=AX.X), [re16], [rsm])
                    S.act(sm[:, 16:24], sm[:, 16:24], AF.Ln, [rsm], [rsm])
                    S.tt("dve", sm[:, 24:32], sm[:, 0:8], sm[:, 8:16], ALU.subtract, [rsm], [rsm])
                    S.tt("dve", sm[:, 32:40], sm[:, 24:32], sm[:, 16:24], ALU.subtract, [rsm], [rsm])
                    S.act(Kf[tt][:], sm[:, 32:40], AF.Exp, [rsm], [rgp[tt]])
                    S.tt("dve", sm[:, 40:48], sm[:, 32:40], sm[:, 0:8], ALU.subtract, [rsm], [rsm])
                    s4 = ssb[tt][:].rearrange("p (a b) c -> p a b c", b=2)
                    S.tt("dve", P1[tt][:], s4[:, :, 0, :], sm[:, 40:48].unsqueeze(2).to_broadcast([128, 8, 128]), ALU.add,
                         [rss[tt], rsm], [rgp[tt]])
                    S.act(P1[tt][:], P1[tt][:], AF.Exp, [rgp[tt]], [rgp[tt]])
                    S.act(P2[tt][:], s4[:, :, 1, :], AF.Exp, [rss[tt]], [rgp[tt]])
                    if ch == 0 and tt == 0:
                        tap("ssb", ssb[0][:].rearrange("p a b -> p (a b)"), [rss[0]])
                        tap("sm", sm[:], [rsm])
                S.op("dve", lambda h: h.memset(fence[:, 1:2], 0.0), alias_rs, [rfb[0], rfb[1]] + alias_rs)
                groups = [(q8, tt) for q8 in range(16) for tt in range(2)]

                def load_quad(q):
                    ub = q % 3
                    S.ld(uq[ub][:], ubf_d[q * 4:(q + 1) * 4].rearrange("a p f -> p a f"), [rub[q * 4 + j] for j in range(4)],
                         [ruq[ub]], ruq[ub])
                    S.ld(vq[ub][:], vbf_d[q * 4:(q + 1) * 4].rearrange("a p f -> p a f"), [rvb[q * 4 + j] for j in range(4)],
                         [rvq[ub]], rvq[ub])

                def emit_xf(q8, tt, fb):
                    ACT_HEADS = (1, 2, 3, 5, 6, 7)
                    a_slot = {h_: i_ for i_, h_ in enumerate(ACT_HEADS)}
                    d_slot = {h_: i_ % 2 for i_, h_ in enumerate([h_ for h_ in range(8) if h_ not in ACT_HEADS])}

                    def xbuf_of(hd):
                        if hd in a_slot:
                            return xa[a_slot[hd]], rxa[a_slot[hd]]
                        return xd[d_slot[hd]], rxd[d_slot[hd]]

                    def xop(hd):
                        xb, rr = xbuf_of(hd)
                        if hd not in a_slot:
                            S.tt("dve", xb[:].rearrange("p (a b) -> p a b", a=8),
                                 P1[tt][:, hd, q8 * 8:(q8 + 1) * 8].unsqueeze(2).to_broadcast([128, 8, 128]),
                                 P2[tt][:, hd, :].unsqueeze(1).to_broadcast([128, 8, 128]), ALU.mult, [rgp[tt]], [rr])
                        else:
                            for i8 in range(8):
                                S.op("act", (lambda o_, i_, m_: lambda h: h.mul(out=o_, in_=i_, mul=m_))(
                                    xb[:, i8 * 128:(i8 + 1) * 128], P2[tt][:, hd, :], P1[tt][:, hd, q8 * 8 + i8:q8 * 8 + i8 + 1]),
                                    [rgp[tt]], [rr], inc=(i8 == 7))

                    def fop(hd):
                        xb, rr = xbuf_of(hd)
                        S.stt(fbuf[fb][:, hd, :], xb[:], Kf[tt][:, hd:hd + 1], xb[:], ALU.is_ge, ALU.mult, [rr, rgp[tt]], [rfb[fb]])
                    for hd in ACT_HEADS:
                        xop(hd)
                    xop(0); xop(4); fop(1); fop(0); fop(2); fop(3); fop(4); fop(5); fop(6); fop(7)

                def emit_pew(q8, tt, fb):
                    if tt == 1 and q8 < 15:
                        load_quad(2 * q8 + 2)
                    for qq in range(2):
                        q = q8 * 2 + qq
                        ub = q % 3
                        for j in range(4):
                            jj = qq * 4 + j
                            for hd in range(8):
                                S.mm(GT[qq][:, j * 128:(j + 1) * 128], fbuf[fb][:, hd, jj * 128:(jj + 1) * 128], identb[:], hd == 0, hd == 7,
                                     [rfb[fb], rC], [rGT[qq]])
                        for j in range(4):
                            for kc in range(8):
                                S.mm(AT[qq][:, j * 128:(j + 1) * 128], uq[ub][:, j, kc * 128:(kc + 1) * 128],
                                     h2c[:, kc, tt * 128:(tt + 1) * 128], kc == 0, kc == 7, [ruq[ub], rh2c], [rAT[qq]])
                    for qq in range(2):
                        S.act(ga[qq][:], AT[qq][:, :], AF.Gelu, [rAT[qq]], [rga[qq]])
                        S.tt("dve", WT[qq][:].rearrange("p a b -> p (a b)"), GT[qq][:, :], ga[qq][:], ALU.mult, [rGT[qq], rga[qq]], [rWT[qq]])
                    for qq in range(2):
                        q = q8 * 2 + qq
                        ub = q % 3
                        for j in range(4):
                            for hf in range(2):
                                S.mm(OUT[tt][:, hf * 512:(hf + 1) * 512], WT[qq][:, j, :], vq[ub][:, j, hf * 512:(hf + 1) * 512],
                                     q == 0 and j == 0, q == 31 and j == 3, [rWT[qq], rvq[ub]], [rOUT[tt]],
                                     inc=(j == 3 and hf == 1))
                    if tt == 1 and q8 < 15:
                        load_quad(2 * q8 + 3)

                load_quad(0)
                load_quad(1)
                for k in range(len(groups) + 1):
                    if k < len(groups):
                        emit_xf(groups[k][0], groups[k][1], k % 2)
                    if k >= 1:
                        emit_pew(groups[k - 1][0], groups[k - 1][1], (k - 1) % 2)
                for tt in range(2):
                    ti = ch * 2 + tt
                    S.ld(x1l[:], x1s_d[ti], [rx1s[ti]], [rx1l], rx1l)
                    S.tt("dve", yo[:], OUT[tt][:, :], gt2[:], ALU.mult, [rOUT[tt], rgt2], [ryo])
                    S.tt("dve", yo[:], yo[:], x1l[:], ALU.add, [ryo, rx1l], [ryo])
                    S.act(jn[:], yo[:], AF.Square, [ryo], [rjn, rfs], accum=fs[:, 0:1])
                    S.act(fs[:, 2:3], fs[:, 0:1], AF.Ln, [rfs], [rfs], bias=EPS, scale=1.0 / D)
                    S.act(fs[:, 2:3], fs[:, 2:3], AF.Exp, [rfs], [rfs], scale=-0.5)
                    S.stt(jn[:], yo[:], fs[:, 2:3], fgb[:], ALU.mult, ALU.mult, [ryo, rfs, rfg, rjn], [rjn])
                    S.ld(out_d[ti * 128:(ti + 1) * 128, :], jn[:], [rjn], [r_out], rjn)
            S.barrier()
        S.barrier()
        S.emit()
        print("[kernel] instructions:", S.nins, {e: len(S.ops[e]) for e in S.ENGS}, flush=True)
    return nc


def _swa_bias():
    qi = np.arange(128)[:, None]
    kj = np.arange(256)[None, :]
    dist = qi + 128 - kj
    valid = (dist >= 0) & (dist < 128)
    slopes = (2.0 ** (-8.0 * (np.arange(8) + 1) / 8)).astype(np.float32)
    b = np.where(valid[None], -slopes[:, None, None] * dist[None].astype(np.float32), np.float32(NEG)).astype(np.float32)
    return np.ascontiguousarray(b.transpose(1, 0, 2).reshape(128, 8 * 256))


def _prep(inputs):
    f = lambda a: np.ascontiguousarray(np.asarray(a, dtype=np.float32))
    x = f(inputs["x"]); c = f(inputs["c"])
    B, SEQ, _ = x.shape
    half = SEQ // 2
    w_in = f(inputs["w_in"])[0]
    qs = w_in[:, 1552:2064].reshape(D, 8, 64)
    order = [0, 4, 1, 5, 2, 6, 3, 7]
    w_in = w_in.copy()
    w_in[:, 1552:2064] = qs[:, order, :].reshape(D, 512)
    pu = f(inputs["peer_u"])[0]
    uT = np.ascontiguousarray(pu.reshape(128, 128, 8, 128).transpose(0, 3, 2, 1)).reshape(128, 128, D)
    common = {
        "w_ada": f(inputs["w_ada"])[0], "b_ada": f(inputs["b_ada"])[0].reshape(1, -1),
        "norm1_g": f(inputs["norm1_g"])[0], "norm2_g": f(inputs["norm2_g"])[0], "w_in": w_in,
        "w_gla_alpha": f(inputs["w_gla_alpha"])[0], "b_gla_alpha": f(inputs["b_gla_alpha"])[0],
        "gla_norm_g": f(inputs["gla_norm_g"])[0], "swa_sinks": f(inputs["swa_sinks"])[0],
        "swa_norm_g": f(inputs["swa_norm_g"])[0], "w_out": f(inputs["w_out"])[0], "w_peer_q": f(inputs["w_peer_q"])[0],
        "peer_keys_1": np.ascontiguousarray(f(inputs["peer_keys_1"])[0].T), "peer_keys_2": np.ascontiguousarray(f(inputs["peer_keys_2"])[0].T),
        "peer_u": uT, "peer_v": f(inputs["peer_v"])[0].reshape(128, 128, D), "final_g": f(inputs["final_g"]),
        "swa_bias": _swa_bias(),
    }
    maps = []
    for core in range(8):
        b, hf = core // 2, core % 2
        m = dict(common)
        m["x"] = np.ascontiguousarray(x[b, hf * half:(hf + 1) * half])
        m["xp"] = np.ascontiguousarray(x[b, 0:half])
        m["flag"] = np.array([float(hf)], np.float32)
        m["c"] = np.ascontiguousarray(c[b].reshape(8, 128).T)
        maps.append(m)
    return maps, B, SEQ, half


def kernel(**inputs):
    maps, B, SEQ, half = _prep(inputs)
    NT = half // 128
    nc = build_nc(NT, NT)
    res = run_bass_kernel_spmd(nc, maps, core_ids=list(range(8)))
    out = np.empty((B, SEQ, D), np.float32)
    for core in range(8):
        b, hf = core // 2, core % 2
        out[b, hf * half:(hf + 1) * half] = res.results[core]["out"]
    return out
```

```python
import contextlib
import numpy as np
import concourse.bass as bass
import concourse.mybir as mybir
from concourse.bass_utils import run_bass_kernel_spmd

F32 = mybir.dt.float32
BF16 = mybir.dt.bfloat16
AF = mybir.ActivationFunctionType
ALU = mybir.AluOpType
AX = mybir.AxisListType

D = 1024
EPS = 1e-6
NEG = -30000.0


class R:
    __slots__ = ("name", "w", "rs")

    def __init__(self, name=""):
        self.name = name
        self.w = None
        self.rs = []


class Sched:
    ENGS = ("pe", "act", "dve", "pool", "sp")

    def __init__(self, nc, stack):
        self.nc = nc
        self.stack = stack
        self.sems = {}
        self.cnt = {}
        self.ops = {e: [] for e in self.ENGS}
        self.known = {e: {} for e in self.ENGS}
        for e in self.ENGS:
            self._mksem(e)
        self.nins = 0

    def _mksem(self, key):
        s = self.stack.enter_context(self.nc.semaphore("s%d" % len(self.sems)))
        self.sems[key] = s
        self.cnt[key] = 0

    def _need(self, eng, dep):
        if dep is None:
            return
        key, val = dep
        if key == eng and val > self.cnt[eng]:
            return
        if self.known[eng].get(key, 0) >= val:
            return
        self.known[eng][key] = val
        self.ops[eng].append(("wait", key, val))

    def _deps(self, eng, reads, writes):
        for r in reads:
            self._need(eng, r.w)
        for r in writes:
            self._need(eng, r.w)
            for d in r.rs:
                self._need(eng, d)

    def _mark(self, me, reads, writes):
        for r in reads:
            r.rs.append(me)
            if len(r.rs) > 24:
                r.rs = r.rs[-24:] if False else r.rs
        for r in writes:
            r.w = me
            r.rs = []

    def op(self, eng, fn, reads=(), writes=(), inc=True):
        self._deps(eng, reads, writes)
        if inc:
            self.cnt[eng] += 1
            self.ops[eng].append(("ins", fn, eng, 1))
            self._mark((eng, self.cnt[eng]), reads, writes)
        else:
            self.ops[eng].append(("ins", fn, eng, 0))
            self._mark((eng, self.cnt[eng] + 1), reads, writes)
        self.nins += 1

    def dma(self, fn, reads=(), writes=(), key=None, eng="sp"):
        self._deps(eng, reads, writes)
        sk = ("d", id(key))
        if sk not in self.sems:
            self._mksem(sk)
        self.cnt[sk] += 16
        self.ops[eng].append(("ins", fn, sk, 16))
        self._mark((sk, self.cnt[sk]), reads, writes)
        self.nins += 1

    def barrier(self):
        for e in self.ENGS:
            for k, v in self.cnt.items():
                if v > 0:
                    self._need(e, (k, v))

    def emit(self):
        nc = self.nc
        sems = self.sems

        def run(eng):
            def body(h):
                for item in self.ops[eng]:
                    if item[0] == "wait":
                        h.wait_ge(sems[item[1]], item[2])
                    elif item[3] == 0:
                        item[1](h)
                    else:
                        item[1](h).then_inc(sems[item[2]], item[3])
            return body

        with nc.Block() as block:
            block.tensor(run("pe"))
            block.scalar(run("act"))
            block.vector(run("dve"))
            block.gpsimd(run("pool"))
            block.sync(run("sp"))

    def mm(self, out, lhsT, rhs, start, stop, rd, wr, inc=None):
        if inc is None:
            inc = bool(stop)
        self.op("pe", lambda h: h.matmul(out, lhsT=lhsT, rhs=rhs, start=start, stop=stop), rd, wr, inc=inc)

    def tr(self, out, in_, ident, rd, wr):
        self.op("pe", lambda h: h.transpose(out=out, in_=in_, identity=ident), rd, wr)

    def act(self, out, in_, func, rd, wr, bias=None, scale=None, accum=None):
        kw = {}
        if bias is not None:
            kw["bias"] = bias
        if scale is not None:
            kw["scale"] = scale
        if accum is not None:
            kw["accum_out"] = accum
        self.op("act", lambda h: h.activation(out=out, in_=in_, func=func, **kw), rd, wr)

    def tt(self, eng, out, a, b, op, rd, wr):
        self.op(eng, lambda h: h.tensor_tensor(out=out, in0=a, in1=b, op=op), rd, wr)

    def ts(self, eng, out, a, s1, op0, rd, wr, s2=None, op1=None):
        if op1 is None:
            self.op(eng, lambda h: h.tensor_scalar(out=out, in0=a, scalar1=s1, scalar2=None, op0=op0), rd, wr)
        else:
            self.op(eng, lambda h: h.tensor_scalar(out=out, in0=a, scalar1=s1, scalar2=s2, op0=op0, op1=op1), rd, wr)

    def stt(self, out, a, scalar, b, op0, op1, rd, wr):
        self.op("dve", lambda h: h.scalar_tensor_tensor(out=out, in0=a, scalar=scalar, in1=b, op0=op0, op1=op1), rd, wr)

    def cp(self, eng, out, in_, rd, wr):
        if eng == "act":
            self.op("act", lambda h: h.copy(out=out, in_=in_), rd, wr)
        else:
            self.op(eng, lambda h: h.tensor_copy(out=out, in_=in_), rd, wr)

    def ld(self, out, in_, rd, wr, key):
        self.dma(lambda h: h.dma_start(out=out, in_=in_), rd, wr, key=key)


def build_nc(NT, NPRE, dbg=None, lvl=9):
    nc = bass.Bass("TRN2", target_bir_lowering=False)
    T = NT * 128
    TP = NPRE * 128
    dram = lambda name, shape, dt=F32, kind="ExternalInput": nc.dram_tensor(name, shape, dt, kind=kind).ap()
    x_d = dram("x", [T, D])
    xp_d = dram("xp", [TP, D])
    flag_d = dram("flag", [1])
    c_d = dram("c", [128, 8])
    wada_d = dram("w_ada", [D, 6 * D])
    bada_d = dram("b_ada", [1, 6 * D])
    n1g_d = dram("norm1_g", [D])
    n2g_d = dram("norm2_g", [D])
    win_d = dram("w_in", [D, 2320])
    walpha_d = dram("w_gla_alpha", [16, 256])
    balpha_d = dram("b_gla_alpha", [256])
    glag_d = dram("gla_norm_g", [512])
    sinks_d = dram("swa_sinks", [8])
    swag_d = dram("swa_norm_g", [512])
    wout_d = dram("w_out", [D, D])
    wq_d = dram("w_peer_q", [D, 2048])
    k1T_d = dram("peer_keys_1", [128, 128])
    k2T_d = dram("peer_keys_2", [128, 128])
    uT_d = dram("peer_u", [128, 128, D])
    v_d = dram("peer_v", [128, 128, D])
    fg_d = dram("final_g", [D])
    swab_d = dram("swa_bias", [128, 8 * 256])
    out_d = dram("out", [T, D], kind="ExternalOutput")
    x1s_d = dram("x1s", [NT, 128, D], kind="Internal")
    h2s_d = dram("h2s", [NT, 128, D], BF16, kind="Internal")
    ubf_d = dram("ubf", [128, 128, D], BF16, kind="Internal")
    vbf_d = dram("vbf", [128, 128, D], BF16, kind="Internal")
    dbg_d = {}
    if dbg:
        for k, shp in dbg.items():
            dbg_d[k] = dram("dbg_" + k, shp, kind="ExternalOutput")

    with contextlib.ExitStack() as st0:
        S = Sched(nc, st0)
        sb0 = lambda name, shape, dt=F32: st0.enter_context(nc.sbuf_tensor(name, shape, dt))
        r_out = R("out")
        r_dbg = R("dbg")

        def tap(name, ap, rd):
            if name in dbg_d:
                S.ld(dbg_d[name], ap, rd, [r_dbg], key=r_dbg)

        identf = sb0("identf", [128, 128]); identb = sb0("identb", [128, 128], BF16)
        onesf = sb0("onesf", [1, 128])
        gt2 = sb0("gt2", [128, D]); fgb = sb0("fgb", [128, D])
        fl = sb0("fl", [128, 1])
        rC = R("consts")
        S.op("pool", lambda h: h.memset(identf[:], 1.0), [], [rC])
        S.op("pool", lambda h: h.affine_select(out=identf[:], in_=identf[:], pattern=[[-1, 128]], compare_op=ALU.is_equal,
                                               fill=0.0, base=0, channel_multiplier=1), [rC], [rC])
        S.cp("pool", identb[:], identf[:], [rC], [rC])
        S.op("pool", lambda h: h.memset(onesf[:], 1.0), [], [rC])
        rfg = R("fgb"); rfl = R("fl")
        S.ld(fgb[:], fg_d.partition_broadcast(128), [], [rfg], rfg)
        S.ld(fl[:], flag_d.partition_broadcast(128), [], [rfl], rfl)
        rgt2 = R("gt2")
        NCV = 4
        cst = [sb0("cst%d" % i, [128, D]) for i in range(NCV)]; rcst = [R("cst%d" % i) for i in range(NCV)]
        cvb = [sb0("cvb%d" % i, [128, D], BF16) for i in range(NCV)]; rcv = [R("cv%d" % i) for i in range(NCV)]
        rub = [R("ub%d" % i) for i in range(128)]; rvb = [R("vb%d" % i) for i in range(128)]
        conv_state = {"ld": 0, "cv": 0}

        def conv_item(n):
            blk, which = n // 2, n % 2
            return ((uT_d, ubf_d, rub) if which == 0 else (v_d, vbf_d, rvb)), blk

        def conv_loads(k):
            for _ in range(k):
                n = conv_state["ld"]
                if n >= 256:
                    return
                (src, dst, rr), blk = conv_item(n)
                s2 = n % NCV
                S.dma((lambda o_, i_: lambda h: h.dma_start(out=o_, in_=i_))(cst[s2][:], src[blk]), [], [rcst[s2]],
                      key=rcst[s2], eng="pool")
                conv_state["ld"] = n + 1

        def conv_steps(k):
            for _ in range(k):
                n = conv_state["cv"]
                if n >= 256:
                    return
                if conv_state["ld"] <= n:
                    conv_loads(n + 1 - conv_state["ld"])
                (src, dst, rr), blk = conv_item(n)
                s2 = n % NCV
                S.cp("dve", cvb[s2][:], cst[s2][:], [rcst[s2]], [rcv[s2]])
                S.dma((lambda o_, i_: lambda h: h.dma_start(out=o_, in_=i_))(dst[blk], cvb[s2][:]), [rcv[s2]], [rr[blk]],
                      key=rcv[s2], eng="pool")
                conv_state["cv"] = n + 1

        with contextlib.ExitStack() as st:
            sb = lambda name, shape, dt=F32: st.enter_context(nc.sbuf_tensor(name, shape, dt))
            ps = lambda name, shape, dt=F32: st.enter_context(nc.psum_tensor(name, shape, dt))
            PA = ps("PA", [128, 1024], BF16); PB = ps("PB", [128, 512]); PC = ps("PC", [128, 512]); PD = ps("PD", [128, 512])
            PE_ = ps("PE", [128, 512]); PF = ps("PF", [128, 512]); PG = ps("PG", [128, 1024], BF16); PH = ps("PH", [128, 512])
            rPA, rPB, rPC, rPDs, rPDd, rPE, rPF, rPG, rPH = [R("P%d" % i) for i in range(9)]

            winb = sb("winb", [128, 8, 2320], BF16); woutb = sb("woutb", [128, 8, D], BF16)
            walb = sb("walb", [16, 256], BF16)
            gs1 = sb("gs1", [128, D]); sh1 = sb("sh1", [128, D]); gt1 = sb("gt1", [128, D])
            gs2 = sb("gs2", [128, D]); sh2 = sb("sh2", [128, D])
            ggla = sb("ggla", [128, 512]); gswa = sb("gswa", [128, 512])
            balb = sb("balb", [128, 256]); sinkb = sb("sinkb", [128, 8])
            swab = sb("swab", [128, 8, 256]); swab0 = sb("swab0", [128, 8, 256])
            tri2 = sb("tri2", [128, 64]); triblk = sb("triblk", [128, 128])
            fm = sb("fm", [128, 1])
            st_setup = contextlib.ExitStack()
            sbs = lambda name, shape, dt=F32: st_setup.enter_context(nc.sbuf_tensor(name, shape, dt))
            stg = sbs("stg", [128, 4096])
            ccol = sbs("ccol", [128, 8]); modrow = sbs("modrow", [1, 6 * D]); badar = sbs("badar", [1, 6 * D])
            rstg = R("stg"); rwin = R("win"); rwout = R("wout"); rwal = R("wal"); rbc = R("bc"); rmod = R("mod")

            for kc in range(8):
                S.ld(stg[:, 0:2320], win_d[kc * 128:(kc + 1) * 128, :], [], [rstg], rstg)
                S.cp("dve" if kc % 2 == 0 else "pool", winb[:, kc, :], stg[:, 0:2320], [rstg], [rwin])
            for kc in range(8):
                S.ld(stg[:, 0:D], wout_d[kc * 128:(kc + 1) * 128, :], [], [rstg], rstg)
                S.cp("dve" if kc % 2 == 0 else "pool", woutb[:, kc, :], stg[:, 0:D], [rstg], [rwout])
            S.ld(stg[0:16, 0:256], walpha_d[:, :], [], [rstg], rstg)
            S.cp("dve", walb[:], stg[0:16, 0:256], [rstg], [rwal])
            S.ld(ggla[:], glag_d.partition_broadcast(128), [], [rbc], rbc)
            S.ld(gswa[:], swag_d.partition_broadcast(128), [], [rbc], rbc)
            S.ld(balb[:], balpha_d.partition_broadcast(128), [], [rbc], rbc)
            S.ld(sinkb[:], sinks_d.partition_broadcast(128), [], [rbc], rbc)
            S.ld(swab[:].rearrange("p a b -> p (a b)"), swab_d[:, :], [], [rbc], rbc)
            S.ld(gs1[:], n1g_d.partition_broadcast(128), [], [rbc], rbc)
            S.ld(gs2[:], n2g_d.partition_broadcast(128), [], [rbc], rbc)
            S.ld(ccol[:], c_d[:, :], [], [rbc], rbc)
            S.ld(badar[:], bada_d[:, :], [], [rbc], rbc)
            S.op("pool", lambda h: h.memset(triblk[:], 1.0), [rC], [rC])
            S.op("pool", lambda h: h.affine_select(out=triblk[:], in_=triblk[:], pattern=[[1, 128]], compare_op=ALU.is_ge,
                                                   fill=0.0, base=0, channel_multiplier=-1), [rC], [rC])
            S.op("pool", lambda h: h.memset(triblk[0:64, 64:128], 0.0), [rC], [rC])
            S.cp("pool", tri2[0:64, :], triblk[0:64, 0:64], [rC], [rC])
            S.cp("pool", tri2[64:128, :], triblk[64:128, 64:128], [rC], [rC])
            S.ts("dve", fm[:], fl[:], -1.0, ALU.add, [rfl], [rC], s2=-NEG, op1=ALU.mult)
            S.ts("dve", swab0[:, :, 0:128], swab[:, :, 0:128], fm[:, 0:1], ALU.add, [rbc, rC], [rC])
            S.cp("dve", swab0[:, :, 128:256], swab[:, :, 128:256], [rbc, rC], [rC])
            S.act(ccol[:], ccol[:], AF.Silu, [rbc], [rbc])
            wada_v = wada_d.rearrange("(kc p) n -> p kc n", p=128)
            for nb in range(12):
                S.ld(stg[:].rearrange("p (a b) -> p a b", a=8), wada_v[:, :, nb * 512:(nb + 1) * 512], [], [rstg], rstg)
                for kc in range(8):
                    S.mm(PB[0:1, :], ccol[:, kc:kc + 1], stg[:, kc * 512:(kc + 1) * 512], kc == 0, kc == 7, [rbc, rstg], [rPB])
                S.tt("dve", modrow[0:1, nb * 512:(nb + 1) * 512], PB[0:1, :], badar[0:1, nb * 512:(nb + 1) * 512], ALU.add,
                     [rPB, rbc], [rmod])
            segs = [sh1, gs1, gt1, sh2, gs2, gt2]
            for sg in range(6):
                for hf in range(2):
                    S.mm(PC[:, :], onesf[0:1, :], modrow[0:1, sg * D + hf * 512: sg * D + (hf + 1) * 512], True, True,
                         [rC, rmod], [rPC])
                    dst = segs[sg][:, hf * 512:(hf + 1) * 512]
                    if sg in (1, 4):
                        S.stt(dst, PC[:, :], 1.0, dst, ALU.add, ALU.mult, [rPC, rbc], [rbc, rgt2])
                    else:
                        S.cp("act", dst, PC[:, :], [rPC], [rbc, rgt2])

            S.barrier()
            st_setup.close()
            xt = [sb("xt%d" % i, [128, D]) for i in range(2)]; rxt = [R("xt0"), R("xt1")]
            jnk = sb("jnk", [128, D]); rjnk = R("jnk")
            t1 = sb("t1", [128, D]); rt1 = R("t1")
            hb = sb("hb", [128, D], BF16); rhb = R("hb")
            hT = sb("hT", [128, 8, 128], BF16); rhT = R("hT")
            st8 = sb("st8", [128, 16]); rst8 = R("st8")
            aT = sb("aT", [16, 128], BF16); raT = R("aT")
            la = sb("la", [128, 256]); rla = R("la")
            eb = sb("eb", [128, 2, 128]); enb = sb("enb", [128, 2, 128]); reb = R("eb")
            qeT = sb("qeT", [128, 2, 128], BF16); keT = sb("keT", [128, 2, 128], BF16); rqe = R("qe"); rke = R("ke")
            ke = sb("ke", [128, 256], BF16); rket = R("ket")
            vgb = sb("vgb", [128, 512], BF16); rvg = R("vg")
            sr = sb("sr", [128, 512]); rsr = R("sr")
            rgl = sb("rgl", [128, 512]); rrgl = R("rgl")
            scm = sb("scm", [128, 4, 64], BF16); rscm = R("scm")
            Sf = sb("Sf", [128, 2, 128]); Sb = sb("Sb", [128, 2, 128], BF16); rS = R("S"); rSb = R("Sb")
            t1g = sb("t1g", [128, 512]); t2g = sb("t2g", [128, 512]); rt1g = R("t1g"); rt2g = R("t2g")
            og = sb("og", [128, 512]); rog = R("og")
            ocat = sb("ocat", [128, D], BF16); roc = R("ocat")
            qsT = sb("qsT", [128, 4, 128], BF16); rqs = R("qs")
            ksT = [sb("ksT%d" % i, [128, 128], BF16) for i in range(2)]; rks = [R("ks0"), R("ks1")]
            vsb = [sb("vsb%d" % i, [128, 128], BF16) for i in range(2)]; rvs = [R("vs0"), R("vs1")]
            sc2 = sb("sc2", [128, 8, 256]); rsc2 = R("sc2")
            pb = sb("pb", [128, 8, 256], BF16); rpb = R("pb")
            pT = sb("pT", [128, 8, 128], BF16); rpT = R("pT")
            osw = sb("osw", [128, 512]); rosw = R("osw")
            ocT = sb("ocT", [128, 8, 128], BF16); rocT = R("ocT")
            x1 = sb("x1", [128, D]); rx1 = R("x1")
            h2b = sb("h2b", [128, D], BF16); rh2b = R("h2b")
            h2T = sb("h2T", [128, 8, 128], BF16); rh2T = R("h2T")
            rx1s = [R("x1s%d" % i) for i in range(NT)]
            rh2s = [R("h2s%d" % i) for i in range(NT)]

            S.op("pool", lambda h: h.memset(Sf[:], 0.0), [], [rS])
            S.op("pool", lambda h: h.memset(Sb[:], 0.0), [], [rSb])
            for i in range(2):
                S.op("pool", (lambda i: lambda h: h.memset(ksT[i][:], 0.0))(i), [], [rks[i]])
                S.op("pool", (lambda i: lambda h: h.memset(vsb[i][:], 0.0))(i), [], [rvs[i]])

            def rstd_cols(dst, src, n, rr):
                S.act(dst, src, AF.Ln, [rr], [rr], bias=EPS, scale=1.0 / n)
                S.act(dst, dst, AF.Exp, [rr], [rr], scale=-0.5)

            def modulate(src, rsrc, gs, sh, dstb, rdst, dstT, rdstT):
                S.act(jnk[:], src, AF.Square, [rsrc], [rjnk, rst8], accum=st8[:, 0:1])
                rstd_cols(st8[:, 2:3], st8[:, 0:1], D, rst8)
                S.stt(t1[:], src, st8[:, 2:3], gs[:], ALU.mult, ALU.mult, [rsrc, rst8, rbc], [rt1])
                S.tt("dve", dstb[:], t1[:], sh[:], ALU.add, [rt1, rbc], [rdst])
                for kc in range(8):
                    S.tr(PA[:, kc * 128:(kc + 1) * 128], dstb[:, kc * 128:(kc + 1) * 128], identb[:], [rdst, rC], [rPA])
                S.cp("act", dstT[:].rearrange("p a b -> p (a b)"), PA[:, :], [rPA], [rdstT])

            def proj_fm(dst, col0, ncols, rw):
                for kc in range(8):
                    S.mm(dst, winb[:, kc, col0:col0 + ncols], hT[:, kc, :], kc == 0, kc == 7, [rwin, rhT], [rw])

            def proj_tm(dst, col0, ncols, rw):
                for kc in range(8):
                    S.mm(dst, hT[:, kc, :], winb[:, kc, col0:col0 + ncols], kc == 0, kc == 7, [rwin, rhT], [rw])

            def tile_step(g, src_d, full, want_kv, own_idx):
                slot = g % 2
                prev = (g + 1) % 2
                xs = xt[g % 2]; rxs = rxt[g % 2]
                S.ld(xs[:], src_d, [], [rxs], rxs)
                modulate(xs[:], rxs, gs1, sh1, hb, rhb, hT, rhT)
                proj_fm(PB[:, 0:128], 256, 128, rPB)
                proj_fm(PB[:, 128:256], 384, 128, rPB)
                if full:
                    proj_fm(PB[:, 256:384], 0, 128, rPB)
                    proj_fm(PB[:, 384:512], 128, 128, rPB)
                proj_fm(PH[0:16, 128:256], 1024, 16, rPH)
                S.cp("dve", aT[:, :], PH[0:16, 128:256], [rPH], [raT])
                if want_kv:
                    proj_fm(PH[:, 0:128], 2064, 128, rPH)
                    S.cp("act", ksT[slot][:], PH[:, 0:128], [rPH], [rks[slot]])
                    proj_tm(PH[:, 256:384], 2192, 128, rPH)
                    S.cp("dve", vsb[slot][:], PH[:, 256:384], [rPH], [rvs[slot]])
                proj_tm(PE_[:, :], 512, 512, rPE)
                S.cp("act", vgb[:], PE_[:, :], [rPE], [rvg])
                S.mm(PF[:, 0:256], aT[0:16, :], walb[0:16, :], True, True, [raT, rwal], [rPF])
                S.tt("dve", la[:], PF[:, 0:256], balb[:], ALU.add, [rPF, rbc], [rla])
                S.act(la[:], la[:], AF.Exp, [rla], [rla], scale=-1.0)
                S.act(la[:], la[:], AF.Ln, [rla], [rla], bias=1.0)
                for p in range(2):
                    S.mm(PF[:, 256 + p * 128:256 + (p + 1) * 128], la[:, p * 128:(p + 1) * 128], triblk[:], True, True,
                         [rla, rC], [rPF])
                S.act(eb[:].rearrange("p a b -> p (a b)"), PF[:, 256:512], AF.Exp, [rPF], [reb], scale=-1.0 / 16.0)
                S.act(enb[:].rearrange("p a b -> p (a b)"), PF[:, 256:512], AF.Exp, [rPF], [reb], scale=1.0 / 16.0)
                S.tt("dve", keT[:].rearrange("p a b -> p (a b)"), PB[:, 0:256], enb[:].rearrange("p a b -> p (a b)"), ALU.mult,
                     [rPB, reb], [rke])
                if full:
                    S.stt(qeT[:].rearrange("p a b -> p (a b)"), PB[:, 256:512], 0.125, eb[:].rearrange("p a b -> p (a b)"),
                          ALU.mult, ALU.mult, [rPB, reb], [rqe])
                for p in range(2):
                    S.tr(PG[:, p * 128:(p + 1) * 128], keT[:, p, :], identb[:], [rke, rC], [rPG])
                S.cp("dve", ke[:], PG[:, 0:256], [rPG], [rket])
                if full:
                    if lvl < 3.01:
                        return
                    proj_fm(PC[:, 0:128], 1552, 128, rPC); proj_fm(PC[:, 128:256], 1680, 128, rPC)
                    proj_fm(PC[:, 256:384], 1808, 128, rPC); proj_fm(PC[:, 384:512], 1936, 128, rPC)
                    S.cp("act", qsT[:].rearrange("p a b -> p (a b)"), PC[:, :], [rPC], [rqs])
                    if lvl < 3.02:
                        return
                    proj_tm(PC[:, :], 1040, 512, rPC)
                    if lvl < 3.021:
                        return
                    S.tt("dve", rgl[:], PC[:, :], ggla[:], ALU.mult, [rPC, rbc], [rrgl])
                    if lvl < 3.022:
                        return
                    S.act(sr[:], PC[:, :], AF.Exp, [rPC, rrgl], [rsr], scale=-1.0)
                    if lvl < 3.023:
                        return
                    S.act(sr[:], sr[:], AF.Ln, [rsr], [rsr], bias=1.0)
                    if lvl < 3.024:
                        return
                    S.act(sr[:], sr[:], AF.Exp, [rsr], [rsr], scale=-1.0)
                    if lvl < 3.03:
                        return
                    for c in range(2):
                        for hh in range(4):
                            p, o = hh // 2, 64 * (hh % 2)
                            S.mm(PD[c * 64:(c + 1) * 64, hh * 64:(hh + 1) * 64], keT[o:o + 64, p, c * 64:(c + 1) * 64],
                                 qeT[o:o + 64, p, c * 64:(c + 1) * 64], True, True, [rke, rqe], [rPDs])
                    S.tt("dve", scm[:], PD[:, 0:256].rearrange("p (a b) -> p a b", a=4),
                         tri2[:].unsqueeze(1).to_broadcast([128, 4, 64]), ALU.mult, [rPDs, rC], [rscm])
                if full and lvl < 3.04:
                    return
                for c in range(2):
                    cs = slice(c * 64, (c + 1) * 64)
                    if full:
                        for hh in range(4):
                            p, o = hh // 2, 64 * (hh % 2)
                            S.mm(PC[cs, hh * 128:(hh + 1) * 128], qeT[o:o + 64, p, cs], Sb[o:o + 64, p, :], True, True,
                                 [rqe, rSb], [rPC])
                            S.mm(PE_[cs, hh * 128:(hh + 1) * 128], scm[cs, hh, :], vgb[cs, hh * 128:(hh + 1) * 128], True, True,
                                 [rscm, rvg], [rPE])
                    for hh in range(4):
                        p, o = hh // 2, 64 * (hh % 2)
                        S.mm(PD[o:o + 64, 256 + p * 128:256 + (p + 1) * 128], ke[cs, hh * 64:(hh + 1) * 64],
                             vgb[cs, hh * 128:(hh + 1) * 128], True, True, [rket, rvg], [rPDd])
                    for p in range(2):
                        el = eb[:, p, c * 64 + 63:c * 64 + 64]
                        S.ts("dve", Sf[:, p, :], Sf[:, p, :], el, ALU.mult, [rS, reb], [rS])
                        S.stt(Sf[:, p, :], PD[:, 256 + p * 128:256 + (p + 1) * 128], el, Sf[:, p, :], ALU.mult, ALU.add,
                              [rPDd, reb, rS], [rS])
                    S.cp("act", Sb[:], Sf[:], [rS], [rSb])
                if not full:
                    return
                if lvl < 3.2:
                    return
                S.cp("act", t2g[:], PC[:, :], [rPC], [rt2g])
                S.tt("dve", og[:], PE_[:, :], t2g[:], ALU.add, [rPE, rt2g], [rog])
                S.act(t1g[:], og[:], AF.Square, [rog], [rt1g])
                S.op("dve", lambda h: h.reduce_sum(out=st8[:, 4:8], in_=t1g[:].rearrange("p (a b) -> p a b", a=4), axis=AX.X),
                     [rt1g], [rst8])
                rstd_cols(st8[:, 12:16], st8[:, 4:8], 128.0, rst8)
                S.tt("dve", t1g[:].rearrange("p (a b) -> p a b", a=4), og[:].rearrange("p (a b) -> p a b", a=4),
                     st8[:, 12:16].unsqueeze(2).to_broadcast([128, 4, 128]), ALU.mult, [rog, rst8, rt1g], [rt1g])
                S.tt("dve", t2g[:], sr[:], rgl[:], ALU.mult, [rsr, rrgl], [rt2g])
                S.tt("dve", ocat[:, 0:512], t1g[:], t2g[:], ALU.mult, [rt1g, rt2g], [roc])
                if lvl < 3.3:
                    return
                bias_t = swab0 if own_idx == 0 else swab
                for pr in range(4):
                    for hl in range(2):
                        hd = pr * 2 + hl
                        cc, o = hd % 4, 64 * (hd // 4)
                        S.mm(PF[:, hl * 256:hl * 256 + 128], qsT[o:o + 64, cc, :], ksT[prev][o:o + 64, :], True, True,
                             [rqs, rks[prev]], [rPF])
                        S.mm(PF[:, hl * 256 + 128:hl * 256 + 256], qsT[o:o + 64, cc, :], ksT[slot][o:o + 64, :], True, True,
                             [rqs, rks[slot]], [rPF])
                    S.stt(sc2[:, pr * 2:pr * 2 + 2, :].rearrange("p a b -> p (a b)"), PF[:, :], 0.125,
                          bias_t[:, pr * 2:pr * 2 + 2, :].rearrange("p a b -> p (a b)"), ALU.mult, ALU.add, [rPF, rbc, rC], [rsc2])
                sw = sb0s["sw"]
                S.op("dve", lambda h: h.reduce_max(out=sw[:, 0:8], in_=sc2[:], axis=AX.X), [rsc2], [rsw])
                S.tt("dve", sw[:, 0:8], sw[:, 0:8], sinkb[:], ALU.max, [rsw, rbc], [rsw])
                S.ts("dve", sw[:, 8:16], sw[:, 0:8], -1.0, ALU.mult, [rsw], [rsw])
                for hd in range(8):
                    S.act(pb[:, hd, :], sc2[:, hd, :], AF.Exp, [rsc2, rsw], [rpb, rsw], bias=sw[:, 8 + hd:9 + hd], scale=1.0,
                          accum=sw[:, 16 + hd:17 + hd])
                S.tt("dve", sw[:, 24:32], sinkb[:], sw[:, 0:8], ALU.subtract, [rsw, rbc], [rsw])
                S.act(sw[:, 24:32], sw[:, 24:32], AF.Exp, [rsw], [rsw])
                S.tt("dve", sw[:, 24:32], sw[:, 24:32], sw[:, 16:24], ALU.add, [rsw], [rsw])
                S.op("dve", lambda h: h.reciprocal(out=sw[:, 32:40], in_=sw[:, 24:32]), [rsw], [rsw])
                if lvl < 3.4:
                    return
                for hg in range(2):
                    for hl in range(4):
                        hd = hg * 4 + hl
                        for hf in range(2):
                            S.tr(PG[:, (hl * 2 + hf) * 128:(hl * 2 + hf + 1) * 128], pb[:, hd, hf * 128:(hf + 1) * 128], identb[:],
                                 [rpb, rC], [rPG])
                    S.cp("act" if hg == 0 else "dve", pT[:].rearrange("p a b -> p (a b)"), PG[:, :], [rPG], [rpT])
                    for hl in range(4):
                        hd = hg * 4 + hl
                        kv = hd // 4
                        S.mm(PH[:, hd * 64:(hd + 1) * 64], pT[:, hl * 2, :], vsb[prev][:, kv * 64:(kv + 1) * 64], True, False,
                             [rpT, rvs[prev]], [rPH])
                        S.mm(PH[:, hd * 64:(hd + 1) * 64], pT[:, hl * 2 + 1, :], vsb[slot][:, kv * 64:(kv + 1) * 64], False, True,
                             [rpT, rvs[slot]], [rPH])
                S.tt("dve", osw[:].rearrange("p (a b) -> p a b", a=8), PH[:, :].rearrange("p (a b) -> p a b", a=8),
                     sw[:, 32:40].unsqueeze(2).to_broadcast([128, 8, 64]), ALU.mult, [rPH, rsw], [rosw])
                S.act(t2g[:], osw[:], AF.Square, [rosw, roc], [rt2g, rsw], accum=sw[:, 40:41])
                rstd_cols(sw[:, 42:43], sw[:, 40:41], 512.0, rsw)
                S.stt(ocat[:, 512:1024], osw[:], sw[:, 42:43], gswa[:], ALU.mult, ALU.mult, [rosw, rsw, rbc], [roc])
                if lvl < 3.5:
                    return
                for kc in range(8):
                    S.tr(PA[:, kc * 128:(kc + 1) * 128], ocat[:, kc * 128:(kc + 1) * 128], identb[:], [roc, rC], [rPA])
                S.cp("act", ocT[:].rearrange("p a b -> p (a b)"), PA[:, :], [rPA], [rocT])
                for hf, (Pq, rq) in enumerate(((PB, rPB), (PC, rPC))):
                    for kc in range(8):
                        S.mm(Pq[:, :], ocT[:, kc, :], woutb[:, kc, hf * 512:(hf + 1) * 512], kc == 0, kc == 7, [rocT, rwout], [rq])
                    S.tt("dve", t1[:, hf * 512:(hf + 1) * 512], Pq[:, :], gt1[:, hf * 512:(hf + 1) * 512], ALU.mult,
                         [rq, rbc], [rt1])
                S.tt("dve", x1[:], t1[:], xs[:], ALU.add, [rt1, rxs], [rx1])
                S.ld(x1s_d[own_idx], x1[:], [rx1], [rx1s[own_idx]], rx1)
                if lvl < 3.6:
                    return
                modulate(x1[:], rx1, gs2, sh2, h2b, rh2b, h2T, rh2T)
                S.ld(h2s_d[own_idx].rearrange("p (a b) -> p a b", a=8), h2T[:], [rh2T], [rh2s[own_idx]], rh2T)
                if own_idx == 0:
                    tap("x1", x1[:], [rx1])
                    tap("ocat", t1g[:], [rt1g])

            print("[kernel] phase A sbuf bytes remaining:", nc.sbuf_bytes_remaining, flush=True)
            sb0s = {"sw": sb("sw", [128, 48])}
            rsw = R("sw")
            xp_v = xp_d.rearrange("(n p) f -> n p f", p=128)
            x_v = x_d.rearrange("(n p) f -> n p f", p=128)
            per_tile = -(-256 // (NPRE + NT))
            per_tile = min(per_tile, NCV)
            for g in range(NPRE if lvl >= 2 else 0):
                conv_loads(per_tile)
                tile_step(g, xp_v[g], False, g == NPRE - 1, None)
                conv_steps(per_tile)
            for p in range(2):
                S.ts("dve", Sf[:, p, :], Sf[:, p, :], fl[:, 0:1], ALU.mult, [rS, rfl], [rS])
            S.cp("pool", Sb[:], Sf[:], [rS], [rSb])
            for i in range(NT if lvl >= 3 else 0):
                conv_loads(per_tile)
                tile_step(NPRE + i, x_v[i], True, True, i)
                conv_steps(per_tile)
            while conv_state["cv"] < 256:
                conv_loads(NCV)
                conv_steps(NCV)
            S.barrier()

        with contextlib.ExitStack() as st:
            sb = lambda name, shape, dt=F32: st.enter_context(nc.sbuf_tensor(name, shape, dt))
            ps = lambda name, shape, dt=F32: st.enter_context(nc.psum_tensor(name, shape, dt))
            OUT = [ps("OUT%d" % i, [128, D]) for i in range(2)]; rOUT = [R("OUT0"), R("OUT1")]
            AT = [ps("AT%d" % i, [128, 512]) for i in range(2)]; rAT = [R("AT0"), R("AT1")]
            GT = [ps("GT%d" % i, [128, 512]) for i in range(2)]; rGT = [R("GT0"), R("GT1")]
            wqb = sb("wqb", [128, 8, 2048], BF16); rwq = R("wq")
            k1b = sb("k1b", [128, 128], BF16); k2b = sb("k2b", [128, 128], BF16); rkk = R("kk")
            stg = cst; rstg = rcst
            for kc2 in range(16):
                kc, hf = kc2 // 2, kc2 % 2
                S.ld(stg[kc2 % 2][:], wq_d[kc * 128:(kc + 1) * 128, hf * 1024:(hf + 1) * 1024], [], [rstg[kc2 % 2]], rstg[kc2 % 2])
                S.cp("dve" if kc2 % 2 == 0 else "pool", wqb[:, kc, hf * 1024:(hf + 1) * 1024], stg[kc2 % 2][:], [rstg[kc2 % 2]], [rwq])
            S.ld(stg[0][:, 0:128], k1T_d[:, :], [], [rstg[0]], rstg[0])
            S.ld(stg[1][:, 0:128], k2T_d[:, :], [], [rstg[1]], rstg[1])
            S.cp("dve", k1b[:], stg[0][:, 0:128], [rstg[0]], [rkk])
            S.cp("dve", k2b[:], stg[1][:, 0:128], [rstg[1]], [rkk])
            h2c = sb("h2c", [128, 8, 256], BF16); rh2c = R("h2c")
            fbuf = [sb("fbuf%d" % i, [128, 8, 1024], BF16) for i in range(2)]; rfb = [R("fb0"), R("fb1")]
            fl0 = fbuf[0][:].rearrange("p a b -> p (a b)"); fl1 = fbuf[1][:].rearrange("p a b -> p (a b)")
            ssb = [fl0[:, tt * 4096:(tt + 1) * 4096].bitcast(F32).rearrange("p (a b) -> p a b", a=16) for tt in range(2)]
            rss = [R("ss0"), R("ss1")]
            qT = fl1[:, 0:4096].rearrange("p (a b) -> p a b", a=16); rqT = R("qT")
            wk = fl1[:, 4096:5248].bitcast(F32); rwk = R("wk")
            cnd = fl1[:, 5248:6400].bitcast(F32); rcnd = R("cnd")
            vv = fl1[:, 6400:7168].bitcast(F32).rearrange("p (a b c) -> p a b c", a=8, b=2); rvv = R("vv")
            ctop = fl1[:, 7168:7552].bitcast(F32).rearrange("p (a b) -> p a b", a=8); rct = R("ctop")
            e16 = fl1[:, 7552:7808].bitcast(F32).rearrange("p (a b) -> p a b", a=8); re16 = R("e16")
            alias_rs = [rss[0], rss[1], rqT, rwk, rcnd, rvv, rct, re16]
            fence = sb("fence", [128, 2]);
            P1 = [sb("P1_%d" % i, [128, 8, 128]) for i in range(2)]; P2 = [sb("P2_%d" % i, [128, 8, 128]) for i in range(2)]
            Kf = [sb("Kf%d" % i, [128, 8]) for i in range(2)]; rgp = [R("gp0"), R("gp1")]
            sm = sb("sm", [128, 64]); rsm = R("sm")
            xd = [sb("xd%d" % i, [128, 1024]) for i in range(2)]; rxd = [R("xd0"), R("xd1")]
            xa = [sb("xa%d" % i, [128, 1024]) for i in range(5)]; rxa = [R("xa%d" % i) for i in range(5)]
            xbuf = [xd[0], xd[1], xa[0]]; rxb = [rxd[0], rxd[1], rxa[0]]
            uq = [sb("uq%d" % i, [128, 4, D], BF16) for i in range(3)]; ruq = [R("uq%d" % i) for i in range(3)]
            vq = [sb("vq%d" % i, [128, 4, D], BF16) for i in range(3)]; rvq = [R("vq%d" % i) for i in range(3)]
            ga = [sb("ga%d" % i, [128, 512]) for i in range(2)]; rga = [R("ga0"), R("ga1")]
            WT = [sb("WT%d" % i, [128, 4, 128], BF16) for i in range(2)]; rWT = [R("WT0"), R("WT1")]
            x1l, yo, jn = xbuf[0], xbuf[1], xbuf[2]
            rx1l, ryo, rjn = rxb[0], rxb[1], rxb[2]
            fs = sb("fs", [128, 8]); rfs = R("fs")

            print("[kernel] phase B sbuf bytes remaining:", nc.sbuf_bytes_remaining, flush=True)
            NCH = NT // 2 if lvl >= 5 else 0
            cnt_e = 0
            cnt_x = 0
            cnt_a = 0
            for ch in range(NCH):
                for tt in range(2):
                    S.ld(h2c[:, :, tt * 128:(tt + 1) * 128], h2s_d[ch * 2 + tt].rearrange("p (a b) -> p a b", a=8),
                         [rh2s[ch * 2 + tt]], [rh2c], rh2c)
                S.op("dve", lambda h: h.memset(fence[:, 0:1], 0.0), [rfb[0], rfb[1]], alias_rs + [rfb[0], rfb[1]])
                for cc in range(16):
                    a = cc % 2
                    for kc in range(8):
                        S.mm(AT[a][:, 0:256], wqb[:, kc, cc * 128:(cc + 1) * 128], h2c[:, kc, :], kc == 0, kc == 7,
                             [rwq, rh2c], [rAT[a]])
                    S.cp("act" if cc % 2 == 0 else "dve", qT[:, cc, :], AT[a][:, 0:256], [rAT[a]], [rqT])
                for tt in range(2):
                    for q4 in range(4):
                        a = q4 % 2
                        for j in range(4):
                            cc = q4 * 4 + j
                            S.mm(GT[a][:, j * 128:(j + 1) * 128], qT[:, cc, tt * 128:(tt + 1) * 128], (k1b if cc % 2 == 0 else k2b)[:],
                                 True, True, [rqT, rkk], [rGT[a]])
                        S.cp("act" if q4 % 2 == 0 else "dve", ssb[tt][:, q4 * 4:(q4 + 1) * 4, :].rearrange("p a b -> p (a b)"),
                             GT[a][:, :], [rGT[a]], [rss[tt]])
                    for hd in range(8):
                        for hf in range(2):
                            src = ssb[tt][:, hd * 2 + hf, :]
                            dst = vv[:, hd, hf, :]
                            S.op("dve", (lambda d_, s_: lambda h: h.max(out=d_, in_=s_))(dst[:, 0:8], src), [rss[tt]], [rvv])
                            S.op("dve", (lambda d_, s_: lambda h: h.match_replace(out=wk[:, 0:128], in_to_replace=d_, in_values=s_,
                                                                                   imm_value=-1e30))(dst[:, 0:8], src), [rss[tt], rvv], [rwk])
                            S.op("dve", (lambda d_: lambda h: h.max(out=d_, in_=wk[:, 0:128]))(dst[:, 8:16]), [rwk], [rvv])
                            S.op("dve", (lambda d_: lambda h: h.match_replace(out=wk[:, 0:128], in_to_replace=d_, in_values=wk[:, 0:128],
                                                                              imm_value=-1e30))(dst[:, 8:16]), [rwk, rvv], [rwk])
                            S.op("dve", (lambda d_: lambda h: h.max(out=d_, in_=wk[:, 0:128]))(dst[:, 16:24]), [rwk], [rvv])
                    for hd in range(8):
                        S.tt("pool", cnd[:].rearrange("p (a b) -> p a b", a=24),
                             vv[:, hd, 0, :].unsqueeze(2).to_broadcast([128, 24, 24]),
                             vv[:, hd, 1, :].unsqueeze(1).to_broadcast([128, 24, 24]), ALU.add, [rvv], [rcnd])
                        dst = ctop[:, hd, :]
                        S.op("dve", (lambda d_: lambda h: h.max(out=d_, in_=cnd[:]))(dst[:, 0:8]), [rcnd], [rct])
                        S.op("dve", (lambda d_: lambda h: h.match_replace(out=wk[:], in_to_replace=d_, in_values=cnd[:],
                                                                          imm_value=-1e30))(dst[:, 0:8]), [rcnd, rct], [rwk])
                        S.op("dve", (lambda d_: lambda h: h.max(out=d_, in_=wk[:]))(dst[:, 8:16]), [rwk], [rct])
                        S.op("dve", (lambda d_: lambda h: h.match_replace(out=wk[:], in_to_replace=d_, in_values=wk[:],
                                                                          imm_value=-1e30))(dst[:, 8:16]), [rwk, rct], [rwk])
                        S.op("dve", (lambda d_: lambda h: h.max(out=d_, in_=wk[:]))(dst[:, 16:24]), [rwk], [rct])
                    S.tt("dve", sm[:, 0:8], ctop[:, :, 15], ctop[:, :, 16], ALU.add, [rct], [rsm])
                    S.ts("dve", sm[:, 0:8], sm[:, 0:8], 0.5, ALU.mult, [rsm], [rsm])
                    S.cp("dve", sm[:, 8:16], ctop[:, :, 0], [rct], [rsm])
                    S.tt("dve", e16[:], ctop[:, :, 0:16], sm[:, 8:16].unsqueeze(2).to_broadcast([128, 8, 16]), ALU.subtract,
                         [rct, rsm], [re16])
                    S.act(e16[:], e16[:], AF.Exp, [re16], [re16])
                    S.op("dve", lambda h: h.reduce_sum(out=sm[:, 16:24], in_=e16[:], axis=AX.X), [re16], [rsm])
                    S.act(sm[:, 16:24], sm[:, 16:24], AF.Ln, [rsm], [rsm])
                    S.tt("dve", sm[:, 24:32], sm[:, 0:8], sm[:, 8:16], ALU.subtract, [rsm], [rsm])
                    S.tt("dve", sm[:, 32:40], sm[:, 24:32], sm[:, 16:24], ALU.subtract, [rsm], [rsm])
                    S.act(Kf[tt][:], sm[:, 32:40], AF.Exp, [rsm], [rgp[tt]])
                    S.tt("dve", sm[:, 40:48], sm[:, 32:40], sm[:, 0:8], ALU.subtract, [rsm], [rsm])
                    s4 = ssb[tt][:].rearrange("p (a b) c -> p a b c", b=2)
                    S.tt("dve", P1[tt][:], s4[:, :, 0, :], sm[:, 40:48].unsqueeze(2).to_broadcast([128, 8, 128]), ALU.add,
                         [rss[tt], rsm], [rgp[tt]])
                    S.act(P1[tt][:], P1[tt][:], AF.Exp, [rgp[tt]], [rgp[tt]])
                    S.act(P2[tt][:], s4[:, :, 1, :], AF.Exp, [rss[tt]], [rgp[tt]])
                    if ch == 0 and tt == 0:
                        tap("ssb", ssb[0][:].rearrange("p a b -> p (a b)"), [rss[0]])
                        tap("sm", sm[:], [rsm])
                S.op("dve", lambda h: h.memset(fence[:, 1:2], 0.0), alias_rs, [rfb[0], rfb[1]] + alias_rs)
                groups = [(q8, tt) for q8 in range(16) for tt in range(2)]

                def load_quad(q):
                    ub = q % 3
                    S.ld(uq[ub][:], ubf_d[q * 4:(q + 1) * 4].rearrange("a p f -> p a f"), [rub[q * 4 + j] for j in range(4)],
                         [ruq[ub]], ruq[ub])
                    S.ld(vq[ub][:], vbf_d[q * 4:(q + 1) * 4].rearrange("a p f -> p a f"), [rvb[q * 4 + j] for j in range(4)],
                         [rvq[ub]], rvq[ub])

                def emit_xf(q8, tt, fb):
                    ACT_HEADS = (1, 2, 3, 5, 7)
                    a_slot = {h_: i_ for i_, h_ in enumerate(ACT_HEADS)}
                    d_slot = {h_: i_ % 2 for i_, h_ in enumerate([h_ for h_ in range(8) if h_ not in ACT_HEADS])}

                    def xbuf_of(hd):
                        if hd in a_slot:
                            return xa[a_slot[hd]], rxa[a_slot[hd]]
                        return xd[d_slot[hd]], rxd[d_slot[hd]]

                    def xop(hd):
                        xb, rr = xbuf_of(hd)
                        if hd not in a_slot:
                            S.tt("dve", xb[:].rearrange("p (a b) -> p a b", a=8),
                                 P1[tt][:, hd, q8 * 8:(q8 + 1) * 8].unsqueeze(2).to_broadcast([128, 8, 128]),
                                 P2[tt][:, hd, :].unsqueeze(1).to_broadcast([128, 8, 128]), ALU.mult, [rgp[tt]], [rr])
                        else:
                            for i8 in range(8):
                                S.op("act", (lambda o_, i_, m_: lambda h: h.mul(out=o_, in_=i_, mul=m_))(
                                    xb[:, i8 * 128:(i8 + 1) * 128], P2[tt][:, hd, :], P1[tt][:, hd, q8 * 8 + i8:q8 * 8 + i8 + 1]),
                                    [rgp[tt]], [rr], inc=(i8 == 7))

                    def fop(hd):
                        xb, rr = xbuf_of(hd)
                        S.stt(fbuf[fb][:, hd, :], xb[:], Kf[tt][:, hd:hd + 1], xb[:], ALU.is_ge, ALU.mult, [rr, rgp[tt]], [rfb[fb]])
                    for hd in ACT_HEADS:
                        xop(hd)
                    xop(0); xop(4); fop(1); fop(0); fop(2); xop(6); fop(3); fop(4); fop(5); fop(6); fop(7)

                def emit_pew(q8, tt, fb):
                    if tt == 1 and q8 < 15:
                        load_quad(2 * q8 + 2)
                    for qq in range(2):
                        q = q8 * 2 + qq
                        ub = q % 3
                        for j in range(4):
                            jj = qq * 4 + j
                            for hd in range(8):
                                S.mm(GT[qq][:, j * 128:(j + 1) * 128], fbuf[fb][:, hd, jj * 128:(jj + 1) * 128], identb[:], hd == 0, hd == 7,
                                     [rfb[fb], rC], [rGT[qq]])
                        for j in range(4):
                            for kc in range(8):
                                S.mm(AT[qq][:, j * 128:(j + 1) * 128], uq[ub][:, j, kc * 128:(kc + 1) * 128],
                                     h2c[:, kc, tt * 128:(tt + 1) * 128], kc == 0, kc == 7, [ruq[ub], rh2c], [rAT[qq]])
                    for qq in range(2):
                        S.act(ga[qq][:], AT[qq][:, :], AF.Gelu, [rAT[qq]], [rga[qq]])
                        S.tt("dve", WT[qq][:].rearrange("p a b -> p (a b)"), GT[qq][:, :], ga[qq][:], ALU.mult, [rGT[qq], rga[qq]], [rWT[qq]])
                    for qq in range(2):
                        q = q8 * 2 + qq
                        ub = q % 3
                        for j in range(4):
                            for hf in range(2):
                                S.mm(OUT[tt][:, hf * 512:(hf + 1) * 512], WT[qq][:, j, :], vq[ub][:, j, hf * 512:(hf + 1) * 512],
                                     q == 0 and j == 0, q == 31 and j == 3, [rWT[qq], rvq[ub]], [rOUT[tt]],
                                     inc=(j == 3 and hf == 1))
                    if tt == 1 and q8 < 15:
                        load_quad(2 * q8 + 3)

                load_quad(0)
                load_quad(1)
                for k in range(len(groups) + 1):
                    if k < len(groups):
                        emit_xf(groups[k][0], groups[k][1], k % 2)
                    if k >= 1:
                        emit_pew(groups[k - 1][0], groups[k - 1][1], (k - 1) % 2)
                for tt in range(2):
                    ti = ch * 2 + tt
                    S.ld(x1l[:], x1s_d[ti], [rx1s[ti]], [rx1l], rx1l)
                    S.tt("dve", yo[:], OUT[tt][:, :], gt2[:], ALU.mult, [rOUT[tt], rgt2], [ryo])
                    S.tt("dve", yo[:], yo[:], x1l[:], ALU.add, [ryo, rx1l], [ryo])
                    S.act(jn[:], yo[:], AF.Square, [ryo], [rjn, rfs], accum=fs[:, 0:1])
                    S.act(fs[:, 2:3], fs[:, 0:1], AF.Ln, [rfs], [rfs], bias=EPS, scale=1.0 / D)
                    S.act(fs[:, 2:3], fs[:, 2:3], AF.Exp, [rfs], [rfs], scale=-0.5)
                    S.stt(jn[:], yo[:], fs[:, 2:3], fgb[:], ALU.mult, ALU.mult, [ryo, rfs, rfg, rjn], [rjn])
                    S.ld(out_d[ti * 128:(ti + 1) * 128, :], jn[:], [rjn], [r_out], rjn)
            S.barrier()
        S.barrier()
        S.emit()
        print("[kernel] instructions:", S.nins, {e: len(S.ops[e]) for e in S.ENGS}, flush=True)
    return nc


def _swa_bias():
    qi = np.arange(128)[:, None]
    kj = np.arange(256)[None, :]
    dist = qi + 128 - kj
    valid = (dist >= 0) & (dist < 128)
    slopes = (2.0 ** (-8.0 * (np.arange(8) + 1) / 8)).astype(np.float32)
    b = np.where(valid[None], -slopes[:, None, None] * dist[None].astype(np.float32), np.float32(NEG)).astype(np.float32)
    return np.ascontiguousarray(b.transpose(1, 0, 2).reshape(128, 8 * 256))


def _prep(inputs):
    f = lambda a: np.ascontiguousarray(np.asarray(a, dtype=np.float32))
    x = f(inputs["x"]); c = f(inputs["c"])
    B, SEQ, _ = x.shape
    half = SEQ // 2
    w_in = f(inputs["w_in"])[0]
    qs = w_in[:, 1552:2064].reshape(D, 8, 64)
    order = [0, 4, 1, 5, 2, 6, 3, 7]
    w_in = w_in.copy()
    w_in[:, 1552:2064] = qs[:, order, :].reshape(D, 512)
    pu = f(inputs["peer_u"])[0]
    uT = np.ascontiguousarray(pu.reshape(128, 128, 8, 128).transpose(0, 3, 2, 1)).reshape(128, 128, D)
    common = {
        "w_ada": f(inputs["w_ada"])[0], "b_ada": f(inputs["b_ada"])[0].reshape(1, -1),
        "norm1_g": f(inputs["norm1_g"])[0], "norm2_g": f(inputs["norm2_g"])[0], "w_in": w_in,
        "w_gla_alpha": f(inputs["w_gla_alpha"])[0], "b_gla_alpha": f(inputs["b_gla_alpha"])[0],
        "gla_norm_g": f(inputs["gla_norm_g"])[0], "swa_sinks": f(inputs["swa_sinks"])[0],
        "swa_norm_g": f(inputs["swa_norm_g"])[0], "w_out": f(inputs["w_out"])[0], "w_peer_q": f(inputs["w_peer_q"])[0],
        "peer_keys_1": np.ascontiguousarray(f(inputs["peer_keys_1"])[0].T), "peer_keys_2": np.ascontiguousarray(f(inputs["peer_keys_2"])[0].T),
        "peer_u": uT, "peer_v": f(inputs["peer_v"])[0].reshape(128, 128, D), "final_g": f(inputs["final_g"]),
        "swa_bias": _swa_bias(),
    }
    maps = []
    for core in range(8):
        b, hf = core // 2, core % 2
        m = dict(common)
        m["x"] = np.ascontiguousarray(x[b, hf * half:(hf + 1) * half])
        m["xp"] = np.ascontiguousarray(x[b, 0:half])
        m["flag"] = np.array([float(hf)], np.float32)
        m["c"] = np.ascontiguousarray(c[b].reshape(8, 128).T)
        maps.append(m)
    return maps, B, SEQ, half


def kernel(**inputs):
    maps, B, SEQ, half = _prep(inputs)
    NT = half // 128
    nc = build_nc(NT, NT)
    res = run_bass_kernel_spmd(nc, maps, core_ids=list(range(8)))
    out = np.empty((B, SEQ, D), np.float32)
    for core in range(8):
        b, hf = core // 2, core % 2
        out[b, hf * half:(hf + 1) * half] = res.results[core]["out"]
    return out
```
